# Optimizing a Trainium2 kernel written in Bass

```python
import math
import jax, jax.numpy as jnp
from jax import lax
import numpy as np

D_MODEL = 1024
BATCH = 4
SEQ = 8192
DEPTH = 2
DEC_BATCH = 2
DEC_SEQ = 16384
PAST_LEN = 128

GRID_W = 64
HEAD_DIM = 64
D_A = D_MODEL // 2
HYENA_ORDER = 2
HYENA_BANDS = 16
HYENA_EMB = 2 * HYENA_BANDS + 1
HYENA_FO = 64
HYENA_DECAY_MIN = math.log(100.0) / 1.5
HYENA_DECAY_MAX = math.log(100.0) / 0.3
H_B = D_MODEL // 4 // HEAD_DIM
D_B = H_B * HEAD_DIM
NA_KR = 8
NA_KC = 16
H_C = D_MODEL // 4 // HEAD_DIM
D_C = H_C * HEAD_DIM
DIL_PATTERNS = ((128, 1), (512, 4), (2048, 16))
ROPE_THETA = 10000.0
N_BRANCH = 3
IN_SPLITS = (3 * D_A, D_A, 3 * D_B, D_B, 3 * D_C, D_C, N_BRANCH * D_MODEL)
N_IN = sum(IN_SPLITS)
DEEPNORM_ALPHA = (2 * DEPTH) ** 0.25
DEEPNORM_BETA = (8 * DEPTH) ** -0.25
LN_EPS = 1e-5
NEG_INF = -1e30
F32 = jnp.float32

kernel_name = "hybrid_hyena_natten_dilated_encoder"


def _layernorm(x):
    xf = x.astype(F32)
    mu = jnp.mean(xf, -1, keepdims=True)
    var = jnp.mean(jnp.square(xf - mu), -1, keepdims=True)
    return (xf - mu) * lax.rsqrt(var + LN_EPS)


def _rope(x):
    L, dh = x.shape[1], x.shape[-1]
    half = dh // 2
    inv = ROPE_THETA ** (-jnp.arange(half, dtype=F32) / half)
    ang = jnp.arange(L, dtype=F32)[:, None] * inv[None, :]
    cos = jnp.cos(ang)[None, :, None, :]
    sin = jnp.sin(ang)[None, :, None, :]
    xf = x.astype(F32)
    x1, x2 = xf[..., :half], xf[..., half:]
    return jnp.concatenate([x1 * cos - x2 * sin, x2 * cos + x1 * sin], -1).astype(x.dtype)


def _hyena_spectrum(L, w1, b1, freq, w2, b2, w3, b3, decay):
    t = jnp.arange(L, dtype=F32) / L
    bands = jnp.arange(1, HYENA_BANDS + 1, dtype=F32)
    ang = 2.0 * math.pi * t[:, None] * bands[None, :]
    z = jnp.concatenate([t[:, None], jnp.cos(ang), jnp.sin(ang)], -1)
    freq = freq.astype(F32)
    h = jnp.sin(freq[0] * (z @ w1.astype(F32) + b1.astype(F32)))
    h = jnp.sin(freq[1] * (h @ w2.astype(F32) + b2.astype(F32)))
    h = (h @ w3.astype(F32) + b3.astype(F32)).reshape(L, 2, HYENA_ORDER, D_A)
    h = h * jnp.exp(-t[:, None, None, None] * jnp.abs(decay.astype(F32)))
    fwd, bwd = h[:, 0], h[:, 1]
    k = jnp.concatenate([fwd, jnp.zeros((1, HYENA_ORDER, D_A), F32), bwd[:0:-1]], 0)
    k = k / (jnp.sum(jnp.abs(k), 0, keepdims=True) + 1e-6)
    return jnp.fft.rfft(k, axis=0)


def _hyena(u, conv_w, conv_b, kf, skip):
    L = u.shape[1]
    up = jnp.pad(u, ((0, 0), (1, 1), (0, 0)))
    uc = up[:, :-2] * conv_w[0] + up[:, 1:-1] * conv_w[1] + up[:, 2:] * conv_w[2] + conv_b
    v, x1, x2 = jnp.split(uc.astype(F32), 3, axis=-1)
    z = v
    skip = skip.astype(F32)
    for o, xg in enumerate((x1, x2)):
        zf = jnp.fft.rfft(z, n=2 * L, axis=1)
        z = jnp.fft.irfft(zf * kf[:, o], n=2 * L, axis=1)[:, :L] + skip[o] * z
        z = xg * z
    return z.astype(u.dtype)


def _neighbourhood_attention(q, k, v, rpb):
    B, L, H, dh = q.shape
    rows = L // GRID_W
    kr = min(NA_KR, rows)
    r = jnp.arange(rows)
    c = jnp.arange(GRID_W)
    row_idx = jnp.clip(r - kr // 2, 0, rows - kr)[:, None] + jnp.arange(kr)[None, :]
    col_start = jnp.clip(c - NA_KC // 2, 0, GRID_W - NA_KC)
    col_ok = (c[None, :] >= col_start[:, None]) & (c[None, :] < col_start[:, None] + NA_KC)
    dr = row_idx - r[:, None] + NA_KR - 1
    dc = jnp.clip(c[None, :] - c[:, None], -(NA_KC - 1), NA_KC - 1) + NA_KC - 1
    bias = rpb[:, dr[:, None, :, None], dc[None, :, None, :]].astype(F32)
    qg = q.reshape(B, rows, GRID_W, H, dh)
    kg = k.reshape(B, rows, GRID_W, H, dh)[:, row_idx]
    vg = v.reshape(B, rows, GRID_W, H, dh)[:, row_idx]
    s = jnp.einsum('brqhd,brikhd->bhrqik', qg, kg, preferred_element_type=F32) * (dh ** -0.5)
    s = jnp.where(col_ok[:, None, :], s + bias[None], NEG_INF)
    p = jax.nn.softmax(s.reshape(B, H, rows, GRID_W, kr * GRID_W), -1).reshape(s.shape)
    o = jnp.einsum('bhrqik,brikhd->brqhd', p.astype(v.dtype), vg, preferred_element_type=F32)
    return o.reshape(B, L, H * dh).astype(q.dtype)


def _dilated_pattern(q, k, v, window, dilation):
    B, L, H, dh = q.shape
    blk = window // (2 * dilation)
    Ld = L // dilation
    nb = -(-Ld // blk)
    Lp = nb * blk

    def sub(x):
        x = x.reshape(B, Ld, dilation, H, dh)
        return jnp.pad(x, ((0, 0), (0, Lp - Ld), (0, 0), (0, 0), (0, 0)))

    def win(x):
        xb = jnp.pad(sub(x), ((0, 0), (blk, blk), (0, 0), (0, 0), (0, 0)))
        xb = xb.reshape(B, nb + 2, blk, dilation, H, dh)
        return jnp.concatenate([xb[:, :-2], xb[:, 1:-1], xb[:, 2:]], axis=2)

    qs = sub(q).reshape(B, nb, blk, dilation, H, dh)
    ks, vs = win(k), win(v)
    qi = jnp.arange(blk)
    ki = jnp.arange(3 * blk)
    kpos = (jnp.arange(nb)[:, None] - 1) * blk + ki[None, :]
    off = ki[None, :] - blk - qi[:, None]
    valid = (jnp.abs(off)[None] <= blk) & (kpos[:, None, :] >= 0) & (kpos[:, None, :] < Ld)
    s = jnp.einsum('bnqjhd,bnkjhd->bnjhqk', qs, ks, preferred_element_type=F32) * (dh ** -0.5)
    s = jnp.where(valid[None, :, None, None], s, NEG_INF)
    m = jnp.max(s, -1, keepdims=True)
    e = jnp.exp(s - m)
    l = jnp.sum(e, -1, keepdims=True)
    o = jnp.einsum('bnjhqk,bnkjhd->bnqjhd', (e / l).astype(v.dtype), vs, preferred_element_type=F32)
    lse = jnp.transpose((m + jnp.log(l))[..., 0], (0, 1, 4, 2, 3))
    o = o.reshape(B, Lp, dilation, H, dh)[:, :Ld].reshape(B, L, H, dh)
    lse = lse.reshape(B, Lp, dilation, H)[:, :Ld].reshape(B, L, H)
    return o, lse


def _dilated_mixture(q, k, v):
    B, L, H, dh = q.shape
    res = [_dilated_pattern(q, k, v, w, d) for (w, d) in DIL_PATTERNS]
    outs = jnp.stack([r[0] for r in res], 0)
    wts = jax.nn.softmax(jnp.stack([r[1] for r in res], 0), axis=0)
    o = jnp.sum(wts[..., None] * outs, 0)
    return o.reshape(B, L, H * dh).astype(q.dtype)


def _layer(x, c, l, w_ada, b_ada, w_in, b_in, hy_conv_w, hy_conv_b, hy_w1, hy_b1, hy_freq,
           hy_w2, hy_b2, hy_w3, hy_b3, hy_decay, hy_skip, na_rpb, w_branch_a, w_branch_b,
           w_branch_c, w_out, ln_g, ln_b):
    B, L, _ = x.shape
    ada = jax.nn.silu(c) @ w_ada[l] + b_ada[l]
    shift, scale, gate = jnp.split(ada, 3, axis=-1)
    h = (_layernorm(x) * (1.0 + scale[:, None]) + shift[:, None]).astype(x.dtype)
    proj = h @ w_in[l] + b_in[l]
    points = np.cumsum(IN_SPLITS)[:-1].tolist()
    a_in, a_z, b_qkv, b_z, c_qkv, c_z, g_all = jnp.split(proj, points, axis=-1)
    kf = _hyena_spectrum(L, hy_w1[l], hy_b1[l], hy_freq[l], hy_w2[l], hy_b2[l], hy_w3[l], hy_b3[l], hy_decay[l])
    y_a = _hyena(a_in, hy_conv_w[l], hy_conv_b[l], kf, hy_skip[l]) * jax.nn.silu(a_z)
    qb, kb, vb = [t.reshape(B, L, H_B, HEAD_DIM) for t in jnp.split(b_qkv, 3, axis=-1)]
    y_b = _neighbourhood_attention(qb, kb, vb, na_rpb[l]) * jax.nn.silu(b_z)
    qc, kc, vc = [t.reshape(B, L, H_C, HEAD_DIM) for t in jnp.split(c_qkv, 3, axis=-1)]
    y_c = _dilated_mixture(_rope(qc), _rope(kc), vc) * jax.nn.silu(c_z)
    g_a, g_b, g_c = jnp.split(jax.nn.sigmoid(g_all), 3, axis=-1)
    merged = g_a * (y_a @ w_branch_a[l]) + g_b * (y_b @ w_branch_b[l]) + g_c * (y_c @ w_branch_c[l])
    sub = (merged @ w_out[l]) * gate[:, None]
    res = DEEPNORM_ALPHA * x + sub
    return (_layernorm(res) * ln_g[l] + ln_b[l]).astype(x.dtype)


def setup_inputs(seed: int = 0) -> dict:
    key = jax.random.key(seed)
    ks = jax.random.split(key, 32)
    D = D_MODEL

    def nrm(k, shape, scale):
        return jax.random.normal(k, shape, F32) * scale

    return {
        "x_prompt": nrm(ks[0], (BATCH, SEQ, D), 1.0),
        "x_sample": nrm(ks[1], (DEC_BATCH, DEC_SEQ, D), 1.0),
        "c_prompt": nrm(ks[2], (BATCH, D), 1.0),
        "c_sample": nrm(ks[3], (DEC_BATCH, D), 1.0),
        "w_ada": nrm(ks[4], (DEPTH, D, 3 * D), D ** -0.5),
        "b_ada": nrm(ks[5], (DEPTH, 3 * D), 0.02),
        "w_in": nrm(ks[6], (DEPTH, D, N_IN), D ** -0.5),
        "b_in": nrm(ks[7], (DEPTH, N_IN), 0.02),
        "hy_conv_w": nrm(ks[8], (DEPTH, 3, 3 * D_A), 3 ** -0.5),
        "hy_conv_b": nrm(ks[9], (DEPTH, 3 * D_A), 0.02),
        "hy_w1": nrm(ks[10], (DEPTH, HYENA_EMB, HYENA_FO), HYENA_EMB ** -0.5),
        "hy_b1": nrm(ks[11], (DEPTH, HYENA_FO), 0.02),
        "hy_freq": 1.0 + nrm(ks[12], (DEPTH, 2, HYENA_FO), 0.1),
        "hy_w2": nrm(ks[13], (DEPTH, HYENA_FO, HYENA_FO), HYENA_FO ** -0.5),
        "hy_b2": nrm(ks[14], (DEPTH, HYENA_FO), 0.02),
        "hy_w3": nrm(ks[15], (DEPTH, HYENA_FO, 2 * HYENA_ORDER * D_A), HYENA_FO ** -0.5),
        "hy_b3": nrm(ks[16], (DEPTH, 2 * HYENA_ORDER * D_A), 0.02),
        "hy_decay": jax.random.uniform(ks[17], (DEPTH, D_A), F32, HYENA_DECAY_MIN, HYENA_DECAY_MAX),
        "hy_skip": nrm(ks[18], (DEPTH, HYENA_ORDER, D_A), 0.5),
        "na_rpb": nrm(ks[19], (DEPTH, H_B, 2 * NA_KR - 1, 2 * NA_KC - 1), 0.1),
        "w_branch_a": nrm(ks[20], (DEPTH, D_A, D), DEEPNORM_BETA * D_A ** -0.5),
        "w_branch_b": nrm(ks[21], (DEPTH, D_B, D), DEEPNORM_BETA * D_B ** -0.5),
        "w_branch_c": nrm(ks[22], (DEPTH, D_C, D), DEEPNORM_BETA * D_C ** -0.5),
        "w_out": nrm(ks[23], (DEPTH, D, D), DEEPNORM_BETA * D ** -0.5),
        "ln_g": 1.0 + nrm(ks[24], (DEPTH, D), 0.02),
        "ln_b": nrm(ks[25], (DEPTH, D), 0.02),
    }


def reference(x_prompt, x_sample, c_prompt, c_sample, w_ada, b_ada, w_in, b_in, hy_conv_w, hy_conv_b,
              hy_w1, hy_b1, hy_freq, hy_w2, hy_b2, hy_w3, hy_b3, hy_decay, hy_skip, na_rpb,
              w_branch_a, w_branch_b, w_branch_c, w_out, ln_g, ln_b):
    y_prompt, y_sample = x_prompt, x_sample
    for l in range(DEPTH):
        y_prompt = _layer(y_prompt, c_prompt, l, w_ada, b_ada, w_in, b_in, hy_conv_w, hy_conv_b,
                          hy_w1, hy_b1, hy_freq, hy_w2, hy_b2, hy_w3, hy_b3, hy_decay, hy_skip, na_rpb,
                          w_branch_a, w_branch_b, w_branch_c, w_out, ln_g, ln_b)
        y_sample = _layer(y_sample, c_sample, l, w_ada, b_ada, w_in, b_in, hy_conv_w, hy_conv_b,
                          hy_w1, hy_b1, hy_freq, hy_w2, hy_b2, hy_w3, hy_b3, hy_decay, hy_skip, na_rpb,
                          w_branch_a, w_branch_b, w_branch_c, w_out, ln_g, ln_b)
    return (y_prompt, y_sample)
```

```python
import math
from contextlib import ExitStack
import numpy as np
import ml_dtypes
import concourse.bass as bass
import concourse.mybir as mybir
from concourse.bass_utils import run_bass_kernel_spmd

F32 = mybir.dt.float32
BF16 = mybir.dt.bfloat16
ALU = mybir.AluOpType
AF = mybir.ActivationFunctionType
AX = mybir.AxisListType

D = 1024
DEPTH = 2
D_A = 512
N_IN = 7168
ALPHA = (2 * DEPTH) ** 0.25
GRID_W = 64
NEG = -30000.0


class Buf:
    __slots__ = ("name", "w", "r")

    def __init__(self, name=""):
        self.name = name
        self.w = None
        self.r = []


class FW:
    NDMA = 8
    SEM_LIMIT = 30000
    SAME_ENGINE_SYNC = True

    def __init__(self, nc, ctx):
        self.nc = nc
        self.ctx = ctx
        self.eng = {"pe": nc.tensor, "act": nc.scalar, "dve": nc.vector,
                    "pool": nc.gpsimd, "sp": nc.sync}
        self.sems = {}
        self.cnt = {}
        self.epoch = {}
        for k in ("pe", "act", "dve", "pool"):
            self.epoch[k] = 0
            self._new_epoch_sem(k, 0)
        self.dq = {}
        for q in ("sp", "act", "pool"):
            lst = []
            for i in range(self.NDMA):
                key = "d_%s%d" % (q, i)
                self.sems[key] = ctx.enter_context(nc.semaphore(key))
                self.cnt[key] = 0
                lst.append(key)
            self.dq[q] = [lst, 0]
        self.known = {e: {} for e in self.eng}
        self.nins = 0
        self.same_engine_sync = FW.SAME_ENGINE_SYNC

    def _new_epoch_sem(self, e, ep):
        key = (e, ep)
        self.sems[key] = self.ctx.enter_context(self.nc.semaphore("s_%s_%d" % (e, ep)))
        self.cnt[key] = 0
        return key

    def _next_sig(self, e, commit):
        key = (e, self.epoch[e])
        if self.cnt[key] >= self.SEM_LIMIT:
            if (e, self.epoch[e] + 1) not in self.sems:
                self._new_epoch_sem(e, self.epoch[e] + 1)
            if commit:
                self.epoch[e] += 1
            key = (e, key[1] + 1)
        if commit:
            self.cnt[key] += 1
            return key, self.cnt[key]
        return key, self.cnt[key] + 1

    def _wait(self, e, dep):
        if dep is None:
            return
        key, val = dep
        if isinstance(key, tuple) and key[0] == e and (e == "pe" or not self.same_engine_sync):
            return
        if self.known[e].get(key, 0) >= val:
            return
        self.eng[e].wait_ge(self.sems[key], val)
        self.known[e][key] = val
        self.nins += 1

    def _deps(self, e, reads, writes):
        for b in reads:
            self._wait(e, b.w)
        for b in writes:
            self._wait(e, b.w)
            for d in b.r:
                self._wait(e, d)

    def _commit(self, sig, reads, writes):
        for b in writes:
            b.w = sig
            b.r = []
        for b in reads:
            b.r.append(sig)
            if len(b.r) > 32:
                best = {}
                for k, v in b.r:
                    if best.get(k, 0) < v:
                        best[k] = v
                b.r = list(best.items())

    def op(self, e, fn, reads=(), writes=(), sig=True):
        self._deps(e, reads, writes)
        ins = fn(self.eng[e])
        self.nins += 1
        if sig:
            key, val = self._next_sig(e, True)
            ins.then_inc(self.sems[key], 1)
            s = (key, val)
        else:
            s = self._next_sig(e, False)
        self._commit(s, reads, writes)
        return ins

    def dma(self, q, out, in_, reads=(), writes=(), **kw):
        lst, i = self.dq[q]
        key = lst[i % len(lst)]
        self.dq[q][1] = i + 1
        self._wait(q, (key, self.cnt[key]))
        self._deps(q, reads, writes)
        self.cnt[key] += 16
        ins = self.eng[q].dma_start(out=out, in_=in_, **kw)
        ins.then_inc(self.sems[key], 16)
        self.nins += 1
        self._commit((key, self.cnt[key]), reads, writes)
        return ins

    def drain(self, e="sp"):
        for k in list(self.sems):
            if self.cnt[k] > 0:
                self._wait(e, (k, self.cnt[k]))

    def barrier(self):
        for e in self.eng:
            for k in list(self.sems):
                if self.cnt[k] > 0 and not (isinstance(k, tuple) and k[0] == e):
                    self._wait(e, (k, self.cnt[k]))


class Ring:
    def __init__(self, tiles):
        self.tiles = tiles
        self.bufs = [Buf() for _ in tiles]
        self.i = 0

    def next(self):
        j = self.i % len(self.tiles)
        self.i += 1
        return self.tiles[j], self.bufs[j]


O_A, O_AZ, O_BQKV, O_BZ, O_CQKV, O_CZ, O_G = 0, 1536, 2048, 2816, 3072, 3840, 4096


def _rot_segs(base):
    segs = []
    for h in range(4):
        segs.append((base + h * 64 + 32, 32, -1.0))
        segs.append((base + h * 64, 32, 1.0))
    return segs


def col_plan():
    qs = 0.125
    p = {}
    p["A"] = [(O_A, 1536, 1.0)]
    p["Z"] = [(O_AZ, 512, 1.0), (O_BZ, 256, 1.0), (O_CZ, 256, 1.0)]
    p["BQK"] = [(O_BQKV, 256, qs), (O_BQKV + 256, 256, 1.0)]
    p["CQK"] = ([(O_CQKV, 256, qs)] + [(s, n, m * qs) for (s, n, m) in _rot_segs(O_CQKV)]
                + [(O_CQKV + 256, 256, 1.0)] + _rot_segs(O_CQKV + 256))
    p["V"] = [(O_BQKV + 512, 256, 1.0), (O_CQKV + 512, 256, 1.0)]
    p["G"] = [(O_G, 3072, 1.0)]
    return p


def seg_len(segs):
    return sum(n for _, n, _ in segs)


class Prog:
    def __init__(self, Ls, depth=DEPTH, debug=False):
        self.Ls = Ls
        self.depth = depth
        self.debug = debug
        self.nc = bass.Bass("TRN2", target_bir_lowering=False)
        self.plan = col_plan()
        self.dbg = {}

    def din(self, name, shape, dt=F32):
        return self.nc.dram_tensor(name, list(shape), dt, kind="ExternalInput").ap()

    def dout(self, name, shape, dt=F32):
        return self.nc.dram_tensor(name, list(shape), dt, kind="ExternalOutput").ap()

    def dscr(self, name, shape, dt):
        if self.debug:
            t = self.nc.dram_tensor(name, list(shape), dt, kind="ExternalOutput").ap()
            self.dbg[name] = t
            return t
        return self.nc.dram_tensor(name, list(shape), dt, kind="Internal").ap()

    def T(self, ph, name, shape, dt):
        self._tn = getattr(self, "_tn", 0) + 1
        return ph.enter_context(self.nc.sbuf_tensor("%s_%d" % (name, self._tn), list(shape), dt))

    def ring(self, ph, name, shape, dt, n):
        return Ring([self.T(ph, name, shape, dt) for _ in range(n)])

    def psn(self):
        j = self._psi % 8
        self._psi += 1
        return self.ps[j], self.bps[j]

    def build(self):
        nc = self.nc
        S = len(self.Ls)
        d = self.depth
        W = {}
        W["w_ada"] = self.din("w_ada", [d, D, 3 * D])
        W["b_ada"] = self.din("b_ada", [d, 3 * D])
        W["w_in"] = self.din("w_in", [d, D, N_IN])
        W["b_in"] = self.din("b_in", [d, N_IN])
        W["hy_conv_w"] = self.din("hy_conv_w", [d, 128, 12, 3])
        W["hy_conv_b"] = self.din("hy_conv_b", [d, 128, 12])
        W["hy_w1"] = self.din("hy_w1", [d, 33, 64])
        W["hy_b1"] = self.din("hy_b1", [d, 64, 1])
        W["hy_freq"] = self.din("hy_freq", [d, 64, 2])
        W["hy_w2"] = self.din("hy_w2", [d, 64, 64])
        W["hy_b2"] = self.din("hy_b2", [d, 64, 1])
        W["hy_w3"] = self.din("hy_w3", [d, 64, 2048])
        W["hy_b3"] = self.din("hy_b3", [d, 128, 16])
        W["hy_decay"] = self.din("hy_decay", [d, 128, 4])
        W["hy_skip"] = self.din("hy_skip", [d, 2, 512])
        W["w_branch_a"] = self.din("w_branch_a", [d, 512, D])
        W["w_branch_b"] = self.din("w_branch_b", [d, 256, D])
        W["w_branch_c"] = self.din("w_branch_c", [d, 256, D])
        W["w_out"] = self.din("w_out", [d, D, D])
        W["ln_g"] = self.din("ln_g", [d, D])
        W["ln_b"] = self.din("ln_b", [d, D])
        self.W = W
        self.X = [self.din("x%d" % s, [L, D]) for s, L in enumerate(self.Ls)]
        self.C = [self.din("c%d" % s, [128, 8]) for s in range(S)]
        self.Y = [self.dout("y%d" % s, [L, D]) for s, L in enumerate(self.Ls)]
        self.rope = [self.din("rope%d" % s, [2, 128, L]) for s, L in enumerate(self.Ls)]
        self.nab = [self.din("nab%d" % s, [d] + list(self.na_tables(L)[0].shape[1:]), BF16) for s, L in enumerate(self.Ls)]
        self.dmask = self.din("dmask", [3, 128, 128], BF16)
        self.hz = [self.din("hz%d" % s, [33, L]) for s, L in enumerate(self.Ls)]
        self.hyf32 = []
        self.hybf = []
        for s, L in enumerate(self.Ls):
            a, b = hyena_tables(L)
            self.hyf32.append(self.din("hyf32_%d" % s, list(a.shape), F32))
            self.hybf.append(self.din("hybf_%d" % s, list(b.shape), BF16))
        self.hyG = self.din("hyG", [128, 1280], BF16)
        Lm = max(self.Ls)
        self.XN = [self.dscr("xn%d" % s, [L, D], F32) for s, L in enumerate(self.Ls)]
        self.AT = self.dscr("at", [1536, Lm], BF16)
        self.ZT = self.dscr("zt", [1024, Lm], BF16)
        self.BQK = self.dscr("bqk", [512, Lm], BF16)
        self.CQK = self.dscr("cqk", [512, Lm], BF16)
        self.VT = self.dscr("vt", [Lm, 512], BF16)
        self.GT = self.dscr("gt", [3072, Lm], BF16)
        self.YT = self.din("yt_in", [1024, Lm], BF16) if getattr(self, "yt_in", False) else self.dscr("yt", [1024, Lm], BF16)
        K1m = 2 * Lm // 128
        self.UT = self.dscr("ut", [1536, Lm], BF16)
        self.FT = self.dscr("ft", [2048, Lm], BF16)
        self.KF = self.dscr("kf", [2, 128, 512, 2 * K1m], BF16)
        self.bAT, self.bZT, self.bBQK, self.bCQK, self.bVT, self.bGT, self.bYT = [Buf() for _ in range(7)]
        self.bXN = [Buf() for _ in self.Ls]

        with ExitStack() as ctx:
            self.ctx = ctx
            self.fw = fw = FW(nc, ctx)
            self.ps = [ctx.enter_context(nc.psum_tensor("ps%d" % i, [128, 512], F32)) for i in range(8)]
            self.bps = [Buf() for _ in range(8)]
            self._psi = 0
            self.identf = self.T(ctx, "identf", [128, 128], F32)
            self.ident = self.T(ctx, "ident", [128, 128], BF16)
            self.onesf = self.T(ctx, "onesf", [128, 128], F32)
            self.onesb = self.T(ctx, "onesb", [128, 128], BF16)
            self.epsc = self.T(ctx, "epsc", [128, 1], F32)
            self.bconst = Buf()
            bc = self.bconst
            fw.op("pool", lambda e: e.memset(self.identf[:], 0.0), writes=[bc])
            fw.op("pool", lambda e: e.affine_select(out=self.identf[:], in_=self.identf[:], pattern=[[-1, 128]],
                                                    compare_op=ALU.not_equal, fill=1.0, base=0,
                                                    channel_multiplier=1), reads=[bc], writes=[bc])
            fw.op("pool", lambda e: e.memset(self.onesf[:], 1.0), writes=[bc])
            fw.op("pool", lambda e: e.memset(self.epsc[:], 1e-5), writes=[bc])
            fw.op("dve", lambda e: e.tensor_copy(out=self.ident[:], in_=self.identf[:]), reads=[bc], writes=[bc])
            fw.op("dve", lambda e: e.tensor_copy(out=self.onesb[:], in_=self.onesf[:]), reads=[bc], writes=[bc])
            junk = self.T(ctx, "junk", [1, 64], F32)
            bj = Buf()
            allin = list(W.values()) + self.rope + self.hz + self.X + self.C + self.hyf32
            for ap in allin:
                flat = ap
                while len(flat.shape) > 2:
                    flat = flat[0]
                if len(flat.shape) == 1:
                    flat = flat.rearrange("(a b) -> a b", a=1)
                nj = min(8, flat.shape[1])
                fw.dma("sp", junk[0:1, 0:nj], flat[0:1, 0:nj], writes=[bj])
            junkb = self.T(ctx, "junkb", [1, 64], BF16)
            for ap in self.nab + [self.dmask, self.hyG] + self.hybf + ([self.YT] if getattr(self, "yt_in", False) else []):
                flat = ap
                while len(flat.shape) > 2:
                    flat = flat[0]
                fw.dma("sp", junkb[0:1, 0:8], flat[0:1, 0:8], writes=[bj])
            fw.barrier()
            for l in range(d):
                for s, L in enumerate(self.Ls):
                    xsrc = self.X[s] if l == 0 else self.XN[s]
                    xdst = self.Y[s] if l == d - 1 else self.XN[s]
                    self.bsrc = Buf() if l == 0 else self.bXN[s]
                    self.bdst = Buf() if l == d - 1 else self.bXN[s]
                    self.layer(l, s, L, xsrc, xdst)
            fw.drain("sp")
            fw.barrier()
        return nc

    def layer(self, l, s, L, xsrc, xdst):
        fw = self.fw
        with ExitStack() as lay:
            self.ada(lay, l, s)
            fw.barrier()
            if getattr(self, "stop_after", "") == "ada":
                return
            for groups in (["A", "Z", "BQK", "CQK", "V"], ["G"]):
                with ExitStack() as ph:
                    self.inproj(ph, l, s, L, xsrc, groups)
                fw.barrier()
            if getattr(self, "stop_after", "") == "inproj":
                return
            if not getattr(self, "skip_mixers", False):
                if not getattr(self, "skip_hyena", False):
                    with ExitStack() as ph:
                        self.hyena(ph, l, s, L)
                    fw.barrier()
                with ExitStack() as ph:
                    self.attn(ph, l, s, L)
                fw.barrier()
            with ExitStack() as ph:
                self.merge(ph, l, s, L, xsrc, xdst)
            fw.barrier()

    def ada(self, lay, l, s):
        fw, nc = self.fw, self.nc
        self.shiftc = self.T(lay, "shiftc", [128, 8], F32)
        self.scale1c = self.T(lay, "scale1c", [128, 8], F32)
        self.gatebc = self.T(lay, "gatebc", [128, D], F32)
        self.bada = Buf()
        with ExitStack() as ph:
            cc = self.T(ph, "cc", [128, 8], F32)
            sc = self.T(ph, "sc", [128, 8], F32)
            bcc = Buf()
            war = self.ring(ph, "wa", [128, 3 * D], F32, 2)
            brow = self.T(ph, "brow", [1, 3 * D], F32)
            arow = self.T(ph, "arow", [1, 3 * D], F32)
            barow = Buf()
            fw.dma("sp", cc[:], self.C[s][:, :], writes=[bcc])
            fw.dma("sp", brow[:], self.W["b_ada"][l:l + 1, :], writes=[barow])
            fw.op("act", lambda e: e.activation(out=sc[:], in_=cc[:], func=AF.Silu), reads=[bcc], writes=[bcc])
            for k in range(8):
                wa, bwa = war.next()
                fw.dma("sp", wa[:], self.W["w_ada"][l, k * 128:(k + 1) * 128, :], writes=[bwa])
                for j in range(6):
                    fw.op("pe", lambda e, j=j, k=k, wa=wa: e.matmul(self.ps[j][0:1, :], lhsT=sc[:, k:k + 1],
                                                                     rhs=wa[:, j * 512:(j + 1) * 512],
                                                                     start=(k == 0), stop=(k == 7)),
                          reads=[bcc, bwa], writes=[self.bps[j]], sig=(j == 5))
            for j in range(6):
                fw.op("dve", lambda e, j=j: e.tensor_tensor(out=arow[:, j * 512:(j + 1) * 512], in0=self.ps[j][0:1, :],
                                                            in1=brow[:, j * 512:(j + 1) * 512], op=ALU.add),
                      reads=[self.bps[j], barow], writes=[barow])
            fw.op("dve", lambda e: e.tensor_scalar_add(out=arow[:, D:2 * D], in0=arow[:, D:2 * D], scalar1=1.0),
                  reads=[barow], writes=[barow])
            pc, bpc = self.ps[6], self.bps[6]
            for c in range(16):
                fw.op("pe", lambda e, c=c: e.matmul(pc[:, c:c + 1], lhsT=arow[0:1, c * 128:(c + 1) * 128],
                                                    rhs=self.onesf[0:1, 0:1], start=True, stop=True),
                      reads=[barow, self.bconst], writes=[bpc], sig=(c == 15))
            fw.op("dve", lambda e: e.tensor_copy(out=self.shiftc[:], in_=pc[:, 0:8]), reads=[bpc], writes=[self.bada])
            fw.op("dve", lambda e: e.tensor_copy(out=self.scale1c[:], in_=pc[:, 8:16]), reads=[bpc], writes=[self.bada])
            for h in range(2):
                pg, bpg = self.ps[h], self.bps[h]
                fw.op("pe", lambda e, h=h, pg=pg: e.matmul(pg[:, :], lhsT=self.onesf[0:1, :],
                                                           rhs=arow[0:1, 2 * D + h * 512:2 * D + (h + 1) * 512],
                                                           start=True, stop=True),
                      reads=[barow, self.bconst], writes=[bpg])
                fw.op("act", lambda e, h=h, pg=pg: e.activation(out=self.gatebc[:, h * 512:(h + 1) * 512], in_=pg[:, :],
                                                                func=AF.Copy), reads=[bpg], writes=[self.bada])
            fw.barrier()

    def inproj(self, ph, l, s, L, xsrc, groups):
        fw, nc, plan = self.fw, self.nc, self.plan
        has_v = "V" in groups
        set1 = "A" in groups
        src_lo, src_hi = (0, O_G) if set1 else (O_G, N_IN)
        nsrc = src_hi - src_lo
        segs = []
        gstart = {}
        dst = 0
        for g in groups:
            gstart[g] = dst
            for (src, n, m) in plan[g]:
                segs.append((dst, src, n, m))
                dst += n
        ncols = dst
        nfm = (ncols - (512 if has_v else 0)) // 128
        vstart = gstart.get("V", 0)
        Wp = self.T(ph, "Wp", [128, 8, ncols], BF16)
        bWp = Buf()
        biasc = self.T(ph, "biasc", [128, nfm], F32)
        biasv = self.T(ph, "biasv", [1, 512], BF16)
        bbias = Buf()
        with ExitStack() as p2:
            stg = self.ring(p2, "stg", [128, nsrc], F32, 2)
            binrow = self.T(p2, "binrow", [1, nsrc], F32)
            bsrc = self.T(p2, "bsrc", [1, nsrc], F32)
            browp = self.T(p2, "browp", [1, ncols], F32)
            brow = Buf()
            fw.dma("sp", binrow[:], self.W["b_in"][l:l + 1, src_lo:src_hi], writes=[brow])
            nb = nsrc // 512
            for k in range(8):
                st, bst = stg.next()
                fw.dma("sp", st[:], self.W["w_in"][l, k * 128:(k + 1) * 128, src_lo:src_hi], writes=[bst])
                for i, (dd, src, n, m) in enumerate(segs):
                    fw.op("dve" if i % 2 == 0 else "pool",
                          lambda e, dd=dd, src=src, n=n, m=m, k=k, st=st: e.tensor_scalar(
                              out=Wp[:, k, dd:dd + n], in0=st[:, src - src_lo:src - src_lo + n],
                              scalar1=self.scale1c[:, k:k + 1], scalar2=float(m), op0=ALU.mult, op1=ALU.mult),
                          reads=[bst, self.bada], writes=[bWp])
                for j in range(nb):
                    fw.op("pe", lambda e, j=j, k=k, st=st: e.matmul(self.ps[j][0:1, :], lhsT=self.shiftc[:, k:k + 1],
                                                                     rhs=st[:, j * 512:(j + 1) * 512],
                                                                     start=(k == 0), stop=(k == 7)),
                          reads=[bst, self.bada], writes=[self.bps[j]], sig=(j == nb - 1))
            for j in range(nb):
                fw.op("dve", lambda e, j=j: e.tensor_tensor(out=bsrc[:, j * 512:(j + 1) * 512], in0=self.ps[j][0:1, :],
                                                            in1=binrow[:, j * 512:(j + 1) * 512], op=ALU.add),
                      reads=[self.bps[j], brow], writes=[brow])
            for (dd, src, n, m) in segs:
                fw.op("dve", lambda e, dd=dd, src=src, n=n, m=m: e.tensor_scalar_mul(
                    out=browp[:, dd:dd + n], in0=bsrc[:, src - src_lo:src - src_lo + n], scalar1=float(m)),
                    reads=[brow], writes=[brow])
            pc, bpc = self.ps[0], self.bps[0]
            for c in range(nfm):
                fw.op("pe", lambda e, c=c: e.matmul(pc[:, c:c + 1], lhsT=browp[0:1, c * 128:(c + 1) * 128],
                                                    rhs=self.onesf[0:1, 0:1], start=True, stop=True),
                      reads=[brow, self.bconst], writes=[bpc], sig=(c == nfm - 1))
            fw.op("dve", lambda e: e.tensor_copy(out=biasc[:], in_=pc[:, 0:nfm]), reads=[bpc], writes=[bbias])
            if has_v:
                fw.op("dve", lambda e: e.tensor_copy(out=biasv[:], in_=browp[:, vstart:vstart + 512]),
                      reads=[brow], writes=[bbias])
            fw.barrier()
        chunks = []
        for g in groups:
            if g == "V":
                continue
            c0 = gstart[g] // 128
            n = seg_len(plan[g]) // 128
            for i in range(n):
                chunks.append((g, c0 + i, i))
        dstmap = {"A": (self.AT, self.bAT, AF.Identity), "Z": (self.ZT, self.bZT, AF.Silu),
                  "BQK": (self.BQK, self.bBQK, AF.Identity), "G": (self.GT, self.bGT, AF.Sigmoid)}
        xr = self.ring(ph, "xr", [128, 4, D], F32, 2)
        xnr = self.ring(ph, "xnr", [128, 4, D], BF16, 2)
        hTr = self.ring(ph, "hTr", [128, 8, 512], BF16, 2)
        outr = self.ring(ph, "outr", [128, 512], BF16, 6)
        vor = self.ring(ph, "vor", [128, 512], BF16, 3)
        far = self.ring(ph, "far", [128, 512], F32, 4)
        roper = self.ring(ph, "roper", [128, 2, 512], F32, 2)
        str_ = self.ring(ph, "lnst", [128, 2, 6], F32, 4)
        mvr = self.ring(ph, "lnmv", [128, 2], F32, 4)
        rsr = self.ring(ph, "lnrs", [128, 1], F32, 4)
        nev = 0
        NTB = L // 512
        lnq = {}
        hq = {}

        def prep_ln(tb):
            t0 = tb * 512
            x, bx = xr.next()
            fw.dma("sp", x[:], xsrc[t0:t0 + 512, :].rearrange("(s p) d -> p s d", p=128), reads=[self.bsrc], writes=[bx])
            xn, bxn = xnr.next()
            for sub in range(4):
                self.ln_norm(x[:, sub, :], bx, xn[:, sub, :], bxn, str_, mvr, rsr)
            lnq[tb] = (xn, bxn)

        def prep_T(tb):
            xn, bxn = lnq.pop(tb)
            hT, bhT = hTr.next()
            for sub in range(4):
                pt, bpt = self.psn()
                pT = pt[:].bitcast(BF16)
                for k in range(8):
                    fw.op("pe", lambda e, k=k, sub=sub, pT=pT, xn=xn: e.transpose(
                        out=pT[:, k * 128:(k + 1) * 128], in_=xn[:, sub, k * 128:(k + 1) * 128], identity=self.ident[:]),
                        reads=[bxn, self.bconst], writes=[bpt], sig=(k == 7))
                fw.op("dve" if sub % 2 == 0 else "act",
                      (lambda e, sub=sub, pT=pT, hT=hT: e.tensor_copy(out=hT[:, :, sub * 128:(sub + 1) * 128],
                                                                      in_=pT[:, 0:1024].rearrange("p (k t) -> p k t", k=8)))
                      if sub % 2 == 0 else
                      (lambda e, sub=sub, pT=pT, hT=hT: e.activation(out=hT[:, :, sub * 128:(sub + 1) * 128],
                                                                     in_=pT[:, 0:1024].rearrange("p (k t) -> p k t", k=8),
                                                                     func=AF.Copy)),
                      reads=[bpt], writes=[bhT])
            hq[tb] = (hT, bhT)

        prep_ln(0)
        prep_T(0)
        for tb in range(NTB):
            t0 = tb * 512
            if tb + 1 < NTB:
                prep_ln(tb + 1)
            hT, bhT = hq.pop(tb)
            if "CQK" in groups:
                rp, brp = roper.next()
                fw.dma("sp", rp[:], self.rope[s][:, :, t0:t0 + 512].rearrange("a p t -> p a t"), writes=[brp])
            pend = {}
            for (g, c, i) in chunks:
                pt, bpt = self.psn()
                for k in range(8):
                    fw.op("pe", lambda e, k=k, c=c, pt=pt, hT=hT: e.matmul(pt[:, :], lhsT=Wp[:, k, c * 128:(c + 1) * 128],
                                                                           rhs=hT[:, k, :], start=(k == 0), stop=(k == 7)),
                          reads=[bWp, bhT], writes=[bpt], sig=(k == 7))
                if g == "CQK":
                    fa, bfa = far.next()
                    fw.op("act", lambda e, c=c, pt=pt, fa=fa: e.activation(out=fa[:], in_=pt[:, :], func=AF.Identity,
                                                                          bias=biasc[:, c:c + 1], scale=1.0),
                          reads=[bpt, bbias], writes=[bfa])
                    j, r = divmod(i, 4)
                    hc = r % 2
                    if r < 2:
                        pend[(j, hc)] = (fa, bfa)
                    else:
                        fq, bfq = pend.pop((j, hc))
                        fw.op("dve", lambda e, fq=fq, rp=rp: e.tensor_tensor(out=fq[:], in0=fq[:], in1=rp[:, 0, :], op=ALU.mult),
                              reads=[bfq, brp], writes=[bfq])
                        fw.op("pool", lambda e, fa=fa, rp=rp: e.tensor_tensor(out=fa[:], in0=fa[:], in1=rp[:, 1, :], op=ALU.mult),
                              reads=[bfa, brp], writes=[bfa])
                        o, bo = outr.next()
                        fw.op("dve", lambda e, fq=fq, fa=fa, o=o: e.tensor_tensor(out=o[:], in0=fq[:], in1=fa[:], op=ALU.add),
                              reads=[bfq, bfa], writes=[bo])
                        row = (j * 2 + hc) * 128
                        fw.dma("pool", self.CQK[row:row + 128, t0:t0 + 512], o[:], reads=[bo], writes=[self.bCQK])
                    continue
                dram, bdram, func = dstmap[g]
                o, bo = outr.next()
                if func == AF.Identity and nev % 2 == 0:
                    fw.op("dve", lambda e, c=c, pt=pt, o=o: e.tensor_scalar(out=o[:], in0=pt[:, :], scalar1=biasc[:, c:c + 1],
                                                                            scalar2=None, op0=ALU.add),
                          reads=[bpt, bbias], writes=[bo])
                else:
                    fw.op("act", lambda e, c=c, pt=pt, o=o, func=func: e.activation(out=o[:], in_=pt[:, :], func=func,
                                                                                   bias=biasc[:, c:c + 1], scale=1.0),
                          reads=[bpt, bbias], writes=[bo])
                nev += 1
                fw.dma("pool", dram[i * 128:(i + 1) * 128, t0:t0 + 512], o[:], reads=[bo], writes=[bdram])
            if has_v:
                for sub in range(4):
                    pt, bpt = self.psn()
                    for k in range(8):
                        fw.op("pe", lambda e, k=k, sub=sub, pt=pt, hT=hT: e.matmul(
                            pt[:, :], lhsT=hT[:, k, sub * 128:(sub + 1) * 128], rhs=Wp[:, k, vstart:vstart + 512],
                            start=(k == 0), stop=False), reads=[bWp, bhT], writes=[bpt], sig=False)
                    fw.op("pe", lambda e, pt=pt: e.matmul(pt[:, :], lhsT=self.onesb[0:1, :], rhs=biasv[0:1, :],
                                                          start=False, stop=True),
                          reads=[bbias, self.bconst], writes=[bpt])
                    vo, bvo = vor.next()
                    fw.op("act", lambda e, pt=pt, vo=vo: e.activation(out=vo[:], in_=pt[:, :], func=AF.Copy),
                          reads=[bpt], writes=[bvo])
                    fw.dma("pool", self.VT[t0 + sub * 128:t0 + (sub + 1) * 128, :], vo[:], reads=[bvo], writes=[self.bVT])
            if tb + 1 < NTB:
                prep_T(tb + 1)

    def ln_norm(self, xin, bxin, xout, bxout, str_, mvr, rsr):
        fw = self.fw
        st, bst = str_.next()
        mv, bmv = mvr.next()
        rs, brs = rsr.next()
        for c in range(2):
            fw.op("dve", lambda e, c=c: e.bn_stats(out=st[:, c, :], in_=xin[:, c * 512:(c + 1) * 512]),
                  reads=[bxin], writes=[bst])
        fw.op("dve", lambda e: e.bn_aggr(out=mv[:], in_=st[:]), reads=[bst], writes=[bmv])
        fw.op("act", lambda e: e.activation(out=rs[:], in_=mv[:, 1:2], func=AF.Sqrt, bias=self.epsc[:, 0:1], scale=1.0),
              reads=[bmv, self.bconst], writes=[brs])
        fw.op("dve", lambda e: e.reciprocal(out=rs[:], in_=rs[:]), reads=[brs], writes=[brs])
        fw.op("dve", lambda e: e.tensor_scalar(out=xout, in0=xin, scalar1=mv[:, 0:1], scalar2=rs[:, 0:1],
                                               op0=ALU.subtract, op1=ALU.mult), reads=[bxin, bmv, brs], writes=[bxout])
        return mv, bmv, rs, brs

    def merge(self, ph, l, s, L, xsrc, xdst):
        fw, nc, W = self.fw, self.nc, self.W
        Wbr = self.T(ph, "Wbr", [128, 8, D], BF16)
        Wo = self.T(ph, "Wo", [128, 8, D], BF16)
        lng = self.T(ph, "lng", [128, D], F32)
        lnb = self.T(ph, "lnb", [128, D], F32)
        bW = Buf()
        with ExitStack() as p2:
            stg = self.ring(p2, "mstg", [128, D], F32, 3)
            srcs = [(W["w_branch_a"], 4), (W["w_branch_b"], 2), (W["w_branch_c"], 2)]
            ci = 0
            for (wsrc, n) in srcs:
                for c in range(n):
                    st, bst = stg.next()
                    fw.dma("sp", st[:], wsrc[l, c * 128:(c + 1) * 128, :], writes=[bst])
                    fw.op("act" if ci % 2 else "dve",
                          (lambda e, st=st, ci=ci: e.activation(out=Wbr[:, ci, :], in_=st[:], func=AF.Copy)) if ci % 2 else
                          (lambda e, st=st, ci=ci: e.tensor_copy(out=Wbr[:, ci, :], in_=st[:])),
                          reads=[bst], writes=[bW])
                    ci += 1
            for c in range(8):
                st, bst = stg.next()
                fw.dma("sp", st[:], W["w_out"][l, c * 128:(c + 1) * 128, :], writes=[bst])
                fw.op("act" if c % 2 else "dve",
                      (lambda e, st=st, c=c: e.activation(out=Wo[:, c, :], in_=st[:], func=AF.Copy)) if c % 2 else
                      (lambda e, st=st, c=c: e.tensor_copy(out=Wo[:, c, :], in_=st[:])),
                      reads=[bst], writes=[bW])
            fw.dma("sp", lng[:], W["ln_g"][l:l + 1, :].partition_broadcast(128), writes=[bW])
            fw.dma("sp", lnb[:], W["ln_b"][l:l + 1, :].partition_broadcast(128), writes=[bW])
            fw.barrier()
        yr = self.ring(ph, "my", [128, 8, 512], BF16, 2)
        zr = self.ring(ph, "mz", [128, 8, 512], BF16, 2)
        ygr = self.ring(ph, "myg", [128, 8, 512], BF16, 2)
        gr = self.ring(ph, "mg", [128, 3, 512], BF16, 3)
        mTr = self.ring(ph, "mT", [128, 8, 512], BF16, 2)
        tmr = self.ring(ph, "mtm", [128, 512], F32, 4)
        xr = self.ring(ph, "mx", [128, D], F32, 3)
        rr = self.ring(ph, "mres", [128, D], F32, 3)
        orr = self.ring(ph, "mout", [128, D], F32, 3)
        str_ = self.ring(ph, "mlnst", [128, 2, 6], F32, 4)
        mvr = self.ring(ph, "mlnmv", [128, 2], F32, 4)
        rsr = self.ring(ph, "mlnrs", [128, 1], F32, 4)
        GTv = self.GT.rearrange("(b c p) t -> c p b t", b=3, c=8)
        for tb in range(L // 512):
            t0 = tb * 512
            y, by = yr.next()
            z, bz = zr.next()
            fw.dma("sp", y[:], self.YT[:, t0:t0 + 512].rearrange("(c p) t -> p c t", p=128), reads=[self.bYT], writes=[by])
            fw.dma("sp", z[:], self.ZT[:, t0:t0 + 512].rearrange("(c p) t -> p c t", p=128), reads=[self.bZT], writes=[bz])
            yg, byg = ygr.next()
            fw.op("pool", lambda e, y=y, z=z, yg=yg: e.tensor_tensor(out=yg[:, 0:4, :], in0=y[:, 0:4, :], in1=z[:, 0:4, :], op=ALU.mult),
                  reads=[by, bz], writes=[byg])
            fw.op("dve", lambda e, y=y, z=z, yg=yg: e.tensor_tensor(out=yg[:, 4:8, :], in0=y[:, 4:8, :], in1=z[:, 4:8, :], op=ALU.mult),
                  reads=[by, bz], writes=[byg])
            mT, bmT = mTr.next()
            for dch in range(8):
                g, bg = gr.next()
                fw.dma("sp", g[:], GTv[dch][:, :, t0:t0 + 512], reads=[self.bGT], writes=[bg])
                pts = []
                for (c0, n) in ((0, 4), (4, 2), (6, 2)):
                    pt, bpt = self.psn()
                    for c in range(n):
                        fw.op("pe", lambda e, c=c, c0=c0, n=n, pt=pt, yg=yg, dch=dch: e.matmul(
                            pt[:, :], lhsT=Wbr[:, c0 + c, dch * 128:(dch + 1) * 128], rhs=yg[:, c0 + c, :],
                            start=(c == 0), stop=(c == n - 1)), reads=[bW, byg], writes=[bpt], sig=(c == n - 1))
                    pts.append((pt, bpt))
                t1, bt1 = tmr.next()
                t2, bt2 = tmr.next()
                fw.op("dve", lambda e, t1=t1, g=g, pt=pts[0][0]: e.tensor_tensor(out=t1[:], in0=pt[:, :], in1=g[:, 0, :], op=ALU.mult),
                      reads=[pts[0][1], bg], writes=[bt1])
                fw.op("dve", lambda e, t2=t2, g=g, pt=pts[1][0]: e.tensor_tensor(out=t2[:], in0=pt[:, :], in1=g[:, 1, :], op=ALU.mult),
                      reads=[pts[1][1], bg], writes=[bt2])
                fw.op("pool", lambda e, t1=t1, t2=t2: e.tensor_tensor(out=t1[:], in0=t1[:], in1=t2[:], op=ALU.add),
                      reads=[bt1, bt2], writes=[bt1])
                fw.op("dve", lambda e, t2=t2, g=g, pt=pts[2][0]: e.tensor_tensor(out=t2[:], in0=pt[:, :], in1=g[:, 2, :], op=ALU.mult),
                      reads=[pts[2][1], bg], writes=[bt2])
                fw.op("pool", lambda e, t1=t1, t2=t2, mT=mT, dch=dch: e.tensor_tensor(out=mT[:, dch, :], in0=t1[:], in1=t2[:], op=ALU.add),
                      reads=[bt1, bt2], writes=[bmT])
            for sub in range(4):
                x, bx = xr.next()
                fw.dma("sp", x[:], xsrc[t0 + sub * 128:t0 + (sub + 1) * 128, :], reads=[self.bsrc], writes=[bx])
                res, bres = rr.next()
                for h in range(2):
                    pt, bpt = self.psn()
                    for k in range(8):
                        fw.op("pe", lambda e, k=k, h=h, pt=pt, mT=mT, sub=sub: e.matmul(
                            pt[:, :], lhsT=mT[:, k, sub * 128:(sub + 1) * 128], rhs=Wo[:, k, h * 512:(h + 1) * 512],
                            start=(k == 0), stop=(k == 7)), reads=[bW, bmT], writes=[bpt], sig=(k == 7))
                    fw.op("dve", lambda e, h=h, pt=pt, res=res: e.tensor_tensor(
                        out=res[:, h * 512:(h + 1) * 512], in0=pt[:, :], in1=self.gatebc[:, h * 512:(h + 1) * 512], op=ALU.mult),
                        reads=[bpt, self.bada], writes=[bres])
                fw.op("dve", lambda e, x=x, res=res: e.scalar_tensor_tensor(out=res[:], in0=x[:], scalar=float(ALPHA), in1=res[:],
                                                                            op0=ALU.mult, op1=ALU.add),
                      reads=[bx, bres], writes=[bres])
                o, bo = orr.next()
                self.ln_norm(res[:], bres, o[:], bo, str_, mvr, rsr)
                fw.op("dve", lambda e, o=o: e.tensor_tensor(out=o[:], in0=o[:], in1=lng[:], op=ALU.mult), reads=[bo, bW], writes=[bo])
                fw.op("pool", lambda e, o=o: e.tensor_tensor(out=o[:], in0=o[:], in1=lnb[:], op=ALU.add), reads=[bo, bW], writes=[bo])
                fw.dma("pool", xdst[t0 + sub * 128:t0 + (sub + 1) * 128, :], o[:], reads=[bo], writes=[self.bdst])

    def bXNw(self, s, xdst):
        return self.bXN[s]

    @staticmethod
    def na_tables(L):
        rows = L // GRID_W
        nb = rows // 2
        kq = np.arange(128)
        klr, kc = kq // 64, kq % 64
        qlr, qc = kq // 64, kq % 64
        col_start = np.clip(qc - 8, 0, GRID_W - 16)
        vcol = (kc[:, None] >= col_start[None, :]) & (kc[:, None] < col_start[None, :] + 16)
        dc = np.clip(kc[:, None] - qc[None, :], -15, 15) + 15
        tiles = []
        index = {}
        blocks = []
        for i in range(nb):
            r = 2 * i + qlr
            start = np.clip(r - 4, 0, rows - 8)
            lst = []
            for p in range(max(0, i - 4), min(nb, i + 5)):
                kr = 2 * p + klr
                vrow = (kr[:, None] >= start[None, :]) & (kr[:, None] < start[None, :] + 8)
                valid = vrow & vcol
                if not valid.any():
                    continue
                dr = np.clip(kr[:, None] - r[None, :] + 7, 0, 14)
                key = (p - i, int(start[0] - 2 * i), int(start[64] - 2 * i))
                if key not in index:
                    index[key] = len(tiles)
                    tiles.append((dr, dc, valid))
                lst.append((p, index[key]))
            blocks.append(lst)
        holder = np.zeros((DEPTH, len(tiles), 4, 128, 128), np.float32)
        return holder, blocks, tiles

    @staticmethod
    def na_values(L, rpb):
        holder, blocks, tiles = Prog.na_tables(L)
        out = np.empty(holder.shape, np.float32)
        for l in range(rpb.shape[0]):
            for t, (dr, dc, valid) in enumerate(tiles):
                for h in range(4):
                    out[l, t, h] = np.where(valid, rpb[l, h][dr, dc], np.float32(NEG))
        return out.astype(ml_dtypes.bfloat16)


def rope_table(L):
    half = 32
    inv = (10000.0 ** (-np.arange(half, dtype=np.float32) / half)).astype(np.float32)
    ang = np.arange(L, dtype=np.float32)[None, :] * inv[:, None]
    cos = np.cos(ang).astype(np.float32)
    sin = np.sin(ang).astype(np.float32)
    c = np.concatenate([cos, cos, cos, cos], 0)
    s_ = np.concatenate([sin, sin, sin, sin], 0)
    return np.stack([c, s_], 0)


def band_masks():
    k = np.arange(128)[:, None]
    q = np.arange(128)[None, :]
    out = []
    for off in (-128, 0, 128):
        d = (k + off) - q
        out.append(np.where(np.abs(d) <= 64, 0.0, NEG))
    return np.stack(out, 0).astype(ml_dtypes.bfloat16)


def hyena_pos(L):
    t = (np.arange(L, dtype=np.float32) / np.float32(L)).astype(np.float32)
    bands = np.arange(1, 17, dtype=np.float32)
    ang = (2.0 * math.pi * t[:, None] * bands[None, :]).astype(np.float32)
    z = np.concatenate([t[:, None], np.cos(ang), np.sin(ang)], -1).astype(np.float32)
    return np.ascontiguousarray(z.T)


def run_window(gen_iter, width):
    pending = iter(gen_iter)
    active = []
    done = False
    while True:
        if not done and len(active) < width:
            try:
                active.append(next(pending))
            except StopIteration:
                done = True
        if not active:
            if done:
                return
            continue
        for g in list(active):
            try:
                next(g)
            except StopIteration:
                active.remove(g)


def _attn_methods():
    def attn_block(self, qT, items, rd_bufs, pTr, post):
        fw = self.fw
        n = len(items)
        sb = [self.psn() for _ in range((n + 3) // 4)]
        pnd, bpn = self.psn()
        pn, pd, bpd = pnd[:, 0:128], pnd[:, 128:256], bpn
        for i, (kT, v, bias) in enumerate(items):
            ps_, bps_ = sb[i // 4]
            c0 = (i % 4) * 128
            fw.op("pe", lambda e, kT=kT, ps_=ps_, c0=c0: e.matmul(ps_[:, c0:c0 + 128], lhsT=kT, rhs=qT, start=True, stop=True),
                  reads=rd_bufs, writes=[bps_])
        yield
        pTs = []
        for i in range(n):
            ps_, bps_ = sb[i // 4]
            c0 = (i % 4) * 128
            pT, bpT = pTr.next()
            fw.op("act", lambda e, ps_=ps_, pT=pT, c0=c0: e.activation(out=pT[:], in_=ps_[:, c0:c0 + 128], func=AF.Exp),
                  reads=[bps_], writes=[bpT])
            pTs.append((pT, bpT))
        yield
        for i, (kT, v, bias) in enumerate(items):
            if bias is not None:
                pT, bpT = pTs[i]
                fw.op("dve" if i % 3 else "pool", lambda e, pT=pT, bias=bias: e.tensor_tensor(out=pT[:], in0=pT[:], in1=bias, op=ALU.mult),
                      reads=rd_bufs + [bpT], writes=[bpT])
        yield
        for i, (kT, v, bias) in enumerate(items):
            pT, bpT = pTs[i]
            fw.op("pe", lambda e, v=v, pT=pT, i=i: e.matmul(pn[0:65, 0:128], lhsT=v, rhs=pT[:], start=(i == 0), stop=(i == n - 1)),
                  reads=rd_bufs + [bpT], writes=[bpn], sig=(i == n - 1))
        yield
        dr, bdr = self.denr.next()
        fw.op("act", lambda e: e.activation(out=dr[64:65, :], in_=pn[64:65, 0:128], func=AF.Copy), reads=[bpn], writes=[bdr])
        fw.op("pe", lambda e: e.matmul(pd[0:64, 0:128], lhsT=self.onesf[64:65, 0:64], rhs=dr[64:65, :], start=True, stop=True),
              reads=[bdr, self.bconst], writes=[bpn])
        yield
        post(pn, bpn, pd, bpd)

    def attn(self, ph, l, s, L):
        fw, nc = self.fw, self.nc
        nblk = L // 128
        _, na_blocks, na_tiles = self.na_tables(L)
        nt = len(na_tiles)
        KT = self.T(ph, "aKT", [64, L], BF16)
        QT = self.T(ph, "aQT", [64, L], BF16)
        bKQ = Buf()
        pTr = self.ring(ph, "apT", [128, 128], BF16, 16)
        self.denr = self.ring(ph, "adenr", [128, 128], F32, 6)
        dm = self.T(ph, "adm", [128, 3, 128], BF16)
        bdm = Buf()
        fw.dma("sp", dm[:], self.dmask.rearrange("a k q -> k a q"), writes=[bdm])
        fw.op("act", lambda e: e.activation(out=dm[:], in_=dm[:], func=AF.Exp), reads=[bdm], writes=[bdm])
        with ExitStack() as pb:
            Vh = self.T(pb, "aVh", [128, nblk, 65], BF16)
            fw.op("pool", lambda e: e.memset(Vh[:, :, 64:65], 1.0), writes=[bKQ])
            nb_ = self.T(pb, "anab", [128, nt, 128], BF16)
            outr = self.ring(pb, "aout", [64, 512], BF16, 3)
            rdr = self.ring(pb, "ard", [64, 128], F32, 4)
            for h in range(4):
                fw.dma("sp", QT[:], self.BQK[h * 64:(h + 1) * 64, 0:L], reads=[self.bBQK], writes=[bKQ])
                fw.dma("sp", KT[:], self.BQK[256 + h * 64:256 + (h + 1) * 64, 0:L], reads=[self.bBQK], writes=[bKQ])
                vsrc = self.VT[0:L, h * 64:(h + 1) * 64].rearrange("(b p) c -> p b c", p=128)
                for b0 in range(0, nblk, 16):
                    b1 = min(nblk, b0 + 16)
                    fw.dma("sp", Vh[:, b0:b1, 0:64], vsrc[:, b0:b1, :], reads=[self.bVT], writes=[bKQ])
                fw.dma("sp", nb_[:], self.nab[s][l, :, h].rearrange("t k q -> k t q"), writes=[bKQ])
                fw.op("act", lambda e: e.activation(out=nb_[:], in_=nb_[:], func=AF.Exp), reads=[bKQ], writes=[bKQ])
                outs = {}

                def na_gens():
                    for i in range(nblk):
                        if i % 4 == 0:
                            outs[i // 4] = outr.next()
                        o, bo = outs[i // 4]
                        items = [(KT[:, p * 128:(p + 1) * 128], Vh[:, p, :], nb_[:, tid, :]) for (p, tid) in na_blocks[i]]

                        def post(pn, bpn, pd, bpd, i=i, o=o, bo=bo):
                            rd, brd = rdr.next()
                            fw.op("dve", lambda e: e.reciprocal(out=rd[:], in_=pd[0:64, 0:128]), reads=[bpd], writes=[brd])
                            fw.op("dve", lambda e: e.tensor_tensor(out=o[:, (i % 4) * 128:(i % 4 + 1) * 128], in0=pn[0:64, 0:128],
                                                                   in1=rd[:], op=ALU.mult), reads=[bpn, bpd, brd], writes=[bo])
                            if i % 4 == 3:
                                fw.dma("pool", self.YT[512 + h * 64:512 + (h + 1) * 64, (i - 3) * 128:(i + 1) * 128], o[:],
                                       reads=[bo], writes=[self.bYT])
                        yield self.attn_block(QT[:, i * 128:(i + 1) * 128], items, [bKQ], pTr, post)
                run_window(na_gens(), 2)
            fw.barrier()
        with ExitStack() as pc:
            RNG = min(2048, L)
            pats = (1, 4, 16)
            Vd = {d: self.T(pc, "aVd%d" % d, [128, d, L // (128 * d), 65], BF16) for d in pats}
            for d in pats:
                fw.op("pool", lambda e, d=d: e.memset(Vd[d][:, :, :, 64:65], 1.0), writes=[bKQ])
            accn = self.T(pc, "accn", [64, RNG], F32)
            accd = self.T(pc, "accd", [64, RNG], F32)
            bacc = Buf()
            yo = self.ring(pc, "ayo", [64, RNG], BF16, 2)
            for h in range(4):
                fw.dma("sp", QT[:], self.CQK[h * 64:(h + 1) * 64, 0:L], reads=[self.bCQK], writes=[bKQ])
                fw.dma("sp", KT[:], self.CQK[256 + h * 64:256 + (h + 1) * 64, 0:L], reads=[self.bCQK], writes=[bKQ])
                for d in pats:
                    src = self.VT[0:L, 256 + h * 64:256 + (h + 1) * 64].rearrange("(m dd) c -> dd m c", dd=d)
                    for j in range(d):
                        vs = src[j].rearrange("(b p) c -> p b c", p=128)
                        nbd = L // (128 * d)
                        for b0 in range(0, nbd, 16):
                            b1 = min(nbd, b0 + 16)
                            fw.dma("sp", Vd[d][:, j, b0:b1, 0:64], vs[:, b0:b1, :], reads=[self.bVT], writes=[bKQ])
                for r0 in range(0, L, RNG):
                    def dil_gens(r0=r0):
                        for d in pats:
                            KTv = KT[:].rearrange("p (m dd) -> p dd m", dd=d)
                            QTv = QT[:].rearrange("p (m dd) -> p dd m", dd=d)
                            nkb = L // (128 * d)
                            av_n = accn[:].rearrange("p (m dd) -> p dd m", dd=d)
                            av_d = accd[:].rearrange("p (m dd) -> p dd m", dd=d)
                            for j in range(d):
                                for qb in range(RNG // (128 * d)):
                                    qbg = r0 // (128 * d) + qb
                                    items = []
                                    for mi, kb in enumerate((qbg - 1, qbg, qbg + 1)):
                                        if kb < 0 or kb >= nkb:
                                            continue
                                        items.append((KTv[:, j, kb * 128:(kb + 1) * 128], Vd[d][:, j, kb, :], dm[:, mi, :]))
                                    on = av_n[:, j, qb * 128:(qb + 1) * 128]
                                    od = av_d[:, j, qb * 128:(qb + 1) * 128]

                                    def post(pn, bpn, pd, bpd, on=on, od=od, d=d):
                                        if d == 1:
                                            fw.op("dve", lambda e: e.tensor_copy(out=on, in_=pn[0:64, 0:128]), reads=[bpn, bpd], writes=[bacc])
                                            fw.op("dve", lambda e: e.tensor_copy(out=od, in_=pd[0:64, 0:128]), reads=[bpd], writes=[bacc])
                                        else:
                                            fw.op("dve", lambda e: e.tensor_tensor(out=on, in0=pn[0:64, 0:128], in1=on, op=ALU.add),
                                                  reads=[bpn, bpd, bacc], writes=[bacc])
                                            fw.op("dve", lambda e: e.tensor_tensor(out=od, in0=pd[0:64, 0:128], in1=od, op=ALU.add),
                                                  reads=[bpd, bacc], writes=[bacc])
                                    yield self.attn_block(QTv[:, j, qbg * 128:(qbg + 1) * 128], items, [bKQ, bdm], pTr, post)
                    run_window(dil_gens(), 4)
                    y, by = yo.next()
                    fw.op("dve", lambda e: e.reciprocal(out=accd[:], in_=accd[:]), reads=[bacc], writes=[bacc])
                    fw.op("dve", lambda e, y=y: e.tensor_tensor(out=y[:], in0=accn[:], in1=accd[:], op=ALU.mult),
                          reads=[bacc], writes=[by])
                    fw.dma("pool", self.YT[768 + h * 64:768 + (h + 1) * 64, r0:r0 + RNG], y[:], reads=[by], writes=[self.bYT])
            fw.barrier()

    Prog.attn_block = attn_block
    Prog.attn = attn


_attn_methods()


TWO_PI = 2.0 * math.pi
MAGIC = 12582912.0


def hy_dims(L):
    N = 2 * L
    K1 = N // 128
    K = L // 128
    nh = max(1, K1 // 128)
    kw = min(K1, 128)
    C = max(1, min(2, 512 // (2 * K1)))
    return N, K1, K, nh, kw, C


def hyena_tables(L):
    N, K1, K, nh, kw, C = hy_dims(L)
    n1 = np.arange(128)[:, None].astype(np.float64)
    k1 = np.arange(K1)[None, :].astype(np.float64)
    F1 = np.zeros((128, 2 * K1), np.float64)
    ang = 2 * np.pi * n1 * k1 / K1
    F1[:, :K1] = np.cos(ang)
    F1[:, K1:] = -np.sin(ang)
    F1[K:, :] = 0.0
    angT = 2 * np.pi * n1 * k1 / N
    Tr, Ti = np.cos(angT), -np.sin(angT)
    TT = np.stack([Tr, Tr], 1)
    TI = np.stack([Ti, Ti], 1)
    p = np.arange(128)[:, None, None].astype(np.float64)
    h = np.arange(nh)[None, :, None].astype(np.float64)
    m2 = np.arange(128)[None, None, :].astype(np.float64)
    kk = h * kw + p
    angc = 2 * np.pi * m2 * kk / N
    cTr, cTi = np.cos(angc), np.sin(angc)
    cTT = np.stack([cTr, cTr], 2)
    cTI = np.stack([cTi, cTi], 2)
    m1 = np.arange(K)[None, None, :].astype(np.float64)
    angh = 2 * np.pi * m1 * kk / K1
    cHr = np.cos(angh) / N
    ncHi = -np.sin(angh) / N
    valid = (np.arange(128) < kw)[:, None, None]
    cHr, ncHi = cHr * valid, ncHi * valid
    f32 = np.concatenate([TT.reshape(128, -1), TI.reshape(128, -1), cTT.reshape(128, -1), cTI.reshape(128, -1)], 1)
    bf = np.concatenate([F1, cHr.reshape(128, -1), ncHi.reshape(128, -1), -cHr.reshape(128, -1)], 1)
    return f32.astype(np.float32), bf.astype(ml_dtypes.bfloat16)


def hyena_g_table():
    a = np.arange(128)[:, None].astype(np.float64)
    b = np.arange(128)[None, :].astype(np.float64)
    ang = 2 * np.pi * a * b / 128
    Gr, Gi = np.cos(ang), -np.sin(ang)
    G4 = np.stack([Gr, Gi, -Gi, -Gr], 1).reshape(128, 512)
    cG = np.concatenate([Gr, -Gi, Gi, Gr, -Gr, Gi], 1)
    return np.concatenate([G4, cG], 1).astype(ml_dtypes.bfloat16)


def layout_weights(raw, depth):
    f = lambda a: np.ascontiguousarray(np.asarray(a, dtype=np.float32)[:depth])
    out = {k: f(raw[k]) for k in ("w_ada", "b_ada", "w_in", "b_in", "hy_w1", "hy_w2", "hy_w3", "hy_skip",
                                  "w_branch_a", "w_branch_b", "w_branch_c", "w_out", "ln_g", "ln_b")}
    d = depth
    out["hy_conv_w"] = np.ascontiguousarray(f(raw["hy_conv_w"]).reshape(d, 3, 12, 128).transpose(0, 3, 2, 1))
    out["hy_conv_b"] = np.ascontiguousarray(f(raw["hy_conv_b"]).reshape(d, 12, 128).transpose(0, 2, 1))
    out["hy_b3"] = np.ascontiguousarray(f(raw["hy_b3"]).reshape(d, 16, 128).transpose(0, 2, 1))
    out["hy_decay"] = np.ascontiguousarray(f(raw["hy_decay"]).reshape(d, 4, 128).transpose(0, 2, 1))
    out["hy_b1"] = f(raw["hy_b1"]).reshape(d, 64, 1)
    out["hy_b2"] = f(raw["hy_b2"]).reshape(d, 64, 1)
    out["hy_freq"] = np.ascontiguousarray(f(raw["hy_freq"]).transpose(0, 2, 1))
    return out


def _hyena_methods():
    def cmul(self, src, tR, tI, outRe, outIm, sel, shape, tmpr, rd, wr, bwr):
        fw = self.fw
        n = 1
        for v in shape[1:]:
            n *= v
        ta_, bta = tmpr.next()
        tb_, btb = tmpr.next()
        names = "abcdefg"[:len(shape) - 1]
        pat = "p (" + " ".join(names) + ") -> p " + " ".join(names)
        kw_ = {nm: v for nm, v in zip(names, shape[1:])}
        P_ = shape[0]
        ta = ta_[:P_, 0:n].rearrange(pat, **kw_)
        tb = tb_[:P_, 0:n].rearrange(pat, **kw_)
        fw.op("dve", lambda e: e.tensor_tensor(out=ta, in0=src, in1=tR, op=ALU.mult), reads=rd, writes=[bta])
        fw.op("dve", lambda e: e.tensor_tensor(out=tb, in0=src, in1=tI, op=ALU.mult), reads=rd, writes=[btb])
        fw.op("pool", lambda e: e.tensor_tensor(out=outRe, in0=sel(ta, 0), in1=sel(tb, 1), op=ALU.subtract),
              reads=[bta, btb], writes=[bwr])
        fw.op("pool", lambda e: e.tensor_tensor(out=outIm, in0=sel(tb, 0), in1=sel(ta, 1), op=ALU.add),
              reads=[bta, btb], writes=[bwr])

    def cprod(self, src, tR, tI, ta, tb, rd, bta, ev=None):
        fw = self.fw
        if ev is not None:
            evt, bev = ev
            fw.op("act", lambda e: e.activation(out=evt, in_=src, func=AF.Copy), reads=rd, writes=[bev])
            src = evt
            rd = list(rd) + [bev]
        fw.op("dve", lambda e: e.tensor_tensor(out=ta, in0=src, in1=tR, op=ALU.mult), reads=rd, writes=[bta])
        fw.op("dve", lambda e: e.tensor_tensor(out=tb, in0=src, in1=tI, op=ALU.mult), reads=rd, writes=[bta])

    def hyena(self, ph, l, s, L):
        fw, nc, W = self.fw, self.nc, self.W
        N, K1, K, nh, kw, C = hy_dims(L)
        nf32 = 2 * 2 * K1 + 2 * nh * 2 * 128
        tf = self.T(ph, "hytf", [128, nf32], F32)
        nbf = 2 * K1 + 3 * nh * K
        tb_ = self.T(ph, "hytb", [128, nbf], BF16)
        tg = self.T(ph, "hytg", [128, 1280], BF16)
        bt = Buf()
        fw.dma("sp", tf[:], self.hyf32[s][:, :], writes=[bt])
        fw.dma("sp", tb_[:], self.hybf[s][:, :], writes=[bt])
        fw.dma("sp", tg[:], self.hyG[:, :], writes=[bt])
        H = {"bt": bt}
        tfb = self.T(ph, "hytfb", [128, nf32], BF16)
        fw.op("dve", lambda e: e.tensor_copy(out=tfb[:], in_=tf[:]), reads=[bt], writes=[bt])
        H["TT"] = tfb[:, 0:2 * K1].rearrange("p (r k) -> p r k", r=2)
        H["TI"] = tfb[:, 2 * K1:4 * K1].rearrange("p (r k) -> p r k", r=2)
        o_ = 4 * K1
        H["cTT"] = tfb[:, o_:o_ + nh * 256].rearrange("p (h r m) -> p h r m", h=nh, r=2)
        H["cTI"] = tfb[:, o_ + nh * 256:o_ + 2 * nh * 256].rearrange("p (h r m) -> p h r m", h=nh, r=2)
        H["F1"] = tb_[:, 0:2 * K1]
        H["cHr"] = tb_[:, 2 * K1:2 * K1 + nh * K].rearrange("p (h m) -> p h m", h=nh)
        H["ncHi"] = tb_[:, 2 * K1 + nh * K:2 * K1 + 2 * nh * K].rearrange("p (h m) -> p h m", h=nh)
        H["Gr"], H["Gi"], H["nGi"], H["nGr"] = [tg[:, i * 128:(i + 1) * 128] for i in range(4)]
        H["cG0"], H["cG1"], H["ncG0"] = tg[:, 512:768], tg[:, 768:1024], tg[:, 1024:1280]
        H["ncHr"] = tb_[:, 2 * K1 + 2 * nh * K:2 * K1 + 3 * nh * K].rearrange("p (h m) -> p h m", h=nh)
        self.H = H
        rinv_bc = self.T(ph, "hyrinv", [128, 2, 512], F32)
        skip_bc = self.T(ph, "hyskip", [128, 2, 512], F32)
        H["rinv"], H["skip"], H["brs"] = rinv_bc, skip_bc, Buf()
        fw.dma("sp", skip_bc[:].rearrange("p o c -> p (o c)"),
               W["hy_skip"][l:l + 1].rearrange("a o c -> a (o c)").partition_broadcast(128), writes=[H["brs"]])
        self.bUT, self.bFT, self.bKF = Buf(), Buf(), Buf()
        with ExitStack() as p2:
            self.hy_filter(p2, l, s, L)
        fw.barrier()
        with ExitStack() as p2:
            self.hy_filter_fft(p2, l, s, L)
        fw.barrier()
        with ExitStack() as p2:
            self.hy_shortconv(p2, l, s, L)
        fw.barrier()
        with ExitStack() as p2:
            self.hy_conv(p2, l, s, L)
        fw.barrier()

    def hy_filter(self, ph, l, s, L):
        fw, nc, W, H = self.fw, self.nc, self.W, self.H
        NT = L // 512
        w1 = self.T(ph, "hw1", [33, 64], F32)
        w2 = self.T(ph, "hw2", [64, 64], F32)
        w3 = self.T(ph, "hw3", [64, 2048], F32)
        b1c = self.T(ph, "hb1", [64, 1], F32)
        b2c = self.T(ph, "hb2", [64, 1], F32)
        fq = self.T(ph, "hfq", [64, 2], F32)
        b3c = self.T(ph, "hb3", [128, 16], F32)
        dec = self.T(ph, "hdec", [128, 4], F32)
        cols = self.T(ph, "hcols", [64, 4], F32)
        stats = self.T(ph, "hstats", [128, 16, NT], F32)
        bw = Buf()
        bstat = Buf()
        fw.dma("sp", w1[:], W["hy_w1"][l], writes=[bw])
        fw.dma("sp", w2[:], W["hy_w2"][l], writes=[bw])
        fw.dma("sp", w3[:], W["hy_w3"][l], writes=[bw])
        fw.dma("sp", b1c[:], W["hy_b1"][l], writes=[bw])
        fw.dma("sp", b2c[:], W["hy_b2"][l], writes=[bw])
        fw.dma("sp", fq[:], W["hy_freq"][l], writes=[bw])
        fw.dma("sp", b3c[:], W["hy_b3"][l], writes=[bw])
        fw.dma("sp", dec[:], W["hy_decay"][l], writes=[bw])
        fw.op("dve", lambda e: e.tensor_scalar_mul(out=cols[:, 0:1], in0=fq[:, 0:1], scalar1=1.0 / TWO_PI), reads=[bw], writes=[bw])
        fw.op("dve", lambda e: e.tensor_tensor(out=cols[:, 1:2], in0=cols[:, 0:1], in1=b1c[:], op=ALU.mult), reads=[bw], writes=[bw])
        fw.op("dve", lambda e: e.tensor_scalar_mul(out=cols[:, 2:3], in0=fq[:, 1:2], scalar1=1.0 / TWO_PI), reads=[bw], writes=[bw])
        fw.op("dve", lambda e: e.tensor_tensor(out=cols[:, 3:4], in0=cols[:, 2:3], in1=b2c[:], op=ALU.mult), reads=[bw], writes=[bw])
        ndec = self.T(ph, "hndec", [128, 4], F32)
        fw.op("dve", lambda e: e.tensor_scalar_mul(out=ndec[:], in0=dec[:], scalar1=-1.0), reads=[bw], writes=[bw])
        fw.op("dve", lambda e: e.tensor_tensor(out=dec[:], in0=dec[:], in1=ndec[:], op=ALU.min), reads=[bw], writes=[bw])
        fw.op("pool", lambda e: e.memset(stats[:], 0.0), writes=[bstat])
        ztr = self.ring(ph, "hzt", [33, 512], F32, 2)
        tbr = self.ring(ph, "htb", [128, 512], F32, 2)
        ur = self.ring(ph, "hu", [64, 512], F32, 3)
        tr_ = self.ring(ph, "ht", [64, 512], F32, 3)
        hr = self.ring(ph, "hh", [64, 512], F32, 4)
        er = self.ring(ph, "hE", [128, 512], F32, 8)
        fr = self.ring(ph, "hf", [128, 512], BF16, 4)

        def sin_layer(pre, bpre, ca, cb):
            u, bu = ur.next()
            t, btt = tr_.next()
            h, bh = hr.next()
            fw.op("dve", lambda e: e.tensor_scalar(out=u[:], in0=pre, scalar1=cols[:, ca:ca + 1], scalar2=cols[:, cb:cb + 1],
                                                   op0=ALU.mult, op1=ALU.add), reads=[bpre, bw], writes=[bu])
            fw.op("pool", lambda e: e.tensor_scalar_add(out=t[:], in0=u[:], scalar1=MAGIC), reads=[bu], writes=[btt])
            fw.op("dve", lambda e: e.scalar_tensor_tensor(out=t[:], in0=t[:], scalar=-MAGIC, in1=u[:], op0=ALU.add, op1=ALU.subtract),
                  reads=[btt, bu], writes=[btt])
            fw.op("act", lambda e: e.activation(out=h[:], in_=t[:], func=AF.Sin, scale=-TWO_PI), reads=[btt], writes=[bh])
            return h, bh

        for ti in range(NT):
            t0 = ti * 512
            zt, bzt = ztr.next()
            tbt, btb = tbr.next()
            fw.dma("sp", zt[:], self.hz[s][:, t0:t0 + 512], writes=[bzt])
            fw.dma("sp", tbt[:], self.hz[s][0:1, t0:t0 + 512].partition_broadcast(128), writes=[btb])
            p1, bp1 = self.psn()
            fw.op("pe", lambda e: e.matmul(p1[0:64, :], lhsT=w1[:, :], rhs=zt[:, :], start=True, stop=True),
                  reads=[bw, bzt], writes=[bp1])
            h1, bh1 = sin_layer(p1[0:64, :], bp1, 0, 1)
            p2_, bp2 = self.psn()
            fw.op("pe", lambda e: e.matmul(p2_[0:64, :], lhsT=w2[:, :], rhs=h1[:, :], start=True, stop=True),
                  reads=[bw, bh1], writes=[bp2])
            h2, bh2 = sin_layer(p2_[0:64, :], bp2, 2, 3)
            Es = []
            for q in range(4):
                E, bE = er.next()
                fw.op("act", lambda e, E=E, q=q: e.activation(out=E[:], in_=tbt[:], func=AF.Exp, scale=dec[:, q:q + 1]),
                      reads=[btb, bw], writes=[bE])
                Es.append((E, bE))
            for q16 in range(16):
                p3, bp3 = self.psn()
                fw.op("pe", lambda e, q16=q16, p3=p3: e.matmul(p3[:, :], lhsT=w3[:, q16 * 128:(q16 + 1) * 128], rhs=h2[:, :],
                                                               start=True, stop=True), reads=[bw, bh2], writes=[bp3])
                E, bE = Es[q16 % 4]
                ft, bft = fr.next()
                fw.op("dve", lambda e, q16=q16, p3=p3, E=E, ft=ft: e.scalar_tensor_tensor(
                    out=ft[:], in0=p3[:, :], scalar=b3c[:, q16:q16 + 1], in1=E[:], op0=ALU.add, op1=ALU.mult),
                    reads=[bp3, bE, bw], writes=[bft])
                if q16 >= 8 and ti == 0:
                    fw.op("pool", lambda e, ft=ft: e.memset(ft[:, 0:1], 0.0), reads=[bft], writes=[bft])
                fw.op("dve", lambda e, q16=q16, ft=ft, ti=ti: e.tensor_reduce(out=stats[:, q16, ti:ti + 1], in_=ft[:], axis=AX.X,
                                                                              op=ALU.add, apply_absolute_value=True),
                      reads=[bft], writes=[bstat])
                fw.dma("act", self.FT[q16 * 128:(q16 + 1) * 128, t0:t0 + 512], ft[:], reads=[bft], writes=[self.bFT])
        S = self.T(ph, "hS", [128, 16], F32)
        Dm = self.T(ph, "hD", [128, 8, 128], F32)
        fw.op("dve", lambda e: e.tensor_reduce(out=S[:], in_=stats[:], axis=AX.X, op=ALU.add), reads=[bstat], writes=[bstat])
        fw.op("dve", lambda e: e.tensor_tensor(out=S[:, 0:8], in0=S[:, 0:8], in1=S[:, 8:16], op=ALU.add), reads=[bstat], writes=[bstat])
        fw.op("dve", lambda e: e.tensor_scalar_add(out=S[:, 0:8], in0=S[:, 0:8], scalar1=1e-6), reads=[bstat], writes=[bstat])
        fw.op("dve", lambda e: e.reciprocal(out=S[:, 0:8], in_=S[:, 0:8]), reads=[bstat], writes=[bstat])
        for j in range(8):
            fw.op("dve", lambda e, j=j: e.tensor_scalar_mul(out=Dm[:, j, :], in0=self.identf[:], scalar1=S[:, j:j + 1]),
                  reads=[bstat, self.bconst], writes=[bstat])
        for o in range(2):
            pt, bpt = self.psn()
            fw.op("pe", lambda e, o=o, pt=pt: e.matmul(pt[:, :], lhsT=self.onesf[:, :],
                                                       rhs=Dm[:, 4 * o:4 * o + 4, :].rearrange("p a b -> p (a b)"),
                                                       start=True, stop=True), reads=[bstat, self.bconst], writes=[bpt])
            fw.op("act", lambda e, o=o, pt=pt: e.activation(out=H["rinv"][:, o, :], in_=pt[:, :], func=AF.Copy),
                  reads=[bpt], writes=[H["brs"]])

    def hy_filter_fft(self, ph, l, s, L):
        fw, nc, H = self.fw, self.nc, self.H
        N, K1, K, nh, kw, C = hy_dims(L)
        CBF = 8
        NG = 4
        sel4 = lambda ap, i: ap[:, :, i, :]
        slots = []
        for g in range(NG):
            slots.append(dict(fg=self.ring(ph, "hfg%d" % g, [128, 2, CBF, 128], BF16, 1),
                              stg=self.ring(ph, "hstg%d" % g, [128, CBF, 2 * K1], BF16, 1),
                              tmpr=self.ring(ph, "hftmp%d" % g, [128, 512], BF16, 12),
                              banks=[(self.ps[2 * g], self.bps[2 * g]), (self.ps[2 * g + 1], self.bps[2 * g + 1])]))
        free = list(range(NG))

        def group(o, c0):
            sl = slots[free.pop(0)]
            fg, bfg = sl["fg"].next()
            for dr in range(2):
                r0 = dr * 1024 + o * 512 + c0
                fw.dma("sp", fg[:K, dr, :, :], self.FT[r0:r0 + CBF, 0:L].rearrange("c (a b) -> a c b", b=128),
                       reads=[self.bFT], writes=[bfg])
            st, bst = sl["stg"].next()
            yield
            for c in range(CBF):
                for dr in range(2):
                    pA, bpA = sl["banks"][dr]
                    fw.op("pe", lambda e, pA=pA, dr=dr, c=c: e.matmul(pA[:, 0:2 * K1], lhsT=fg[:K, dr, c, :], rhs=H["F1"][:K, :],
                                                                       start=True, stop=True),
                          reads=[bfg, H["bt"]], writes=[bpA])
                yield
                prods = []
                bpr_ = Buf()
                for dr in range(2):
                    pA, bpA = sl["banks"][dr]
                    ta_, _b1 = sl["tmpr"].next()
                    tb__, _b2 = sl["tmpr"].next()
                    src = pA[:, 0:2 * K1].rearrange("p (r k) -> p r k", r=2)
                    tav = ta_[:, 0:2 * K1].rearrange("p (r k) -> p r k", r=2)
                    tbv = tb__[:, 0:2 * K1].rearrange("p (r k) -> p r k", r=2)
                    ev_, _b3 = sl["tmpr"].next()
                    self.cprod(src, H["TT"], H["TI"], tav, tbv, [bpA, H["bt"], _b1, _b2], bpr_,
                               ev=(ev_[:, 0:2 * K1].rearrange("p (r k) -> p r k", r=2), _b3))
                    _b1.w = _b2.w = bpr_.w
                    prods.append((tav, tbv, _b1, _b2))
                yield
                pK, bpK = sl["banks"][0]
                (taf, tbf, b1, b2), (tab, tbb, b3, b4) = prods
                seq = [("Gr", taf, 0, 0), ("Gr", tab, 0, 0), ("nGr", tbf, 1, 0), ("nGr", tbb, 1, 0),
                       ("nGi", tbf, 0, 0), ("nGi", taf, 1, 0), ("nGi", tbb, 0, 0), ("nGi", tab, 1, 0),
                       ("Gi", taf, 0, 1), ("nGi", tbf, 1, 1), ("Gr", tbf, 0, 1), ("Gr", taf, 1, 1),
                       ("nGi", tab, 0, 1), ("Gi", tbb, 1, 1), ("nGr", tbb, 0, 1), ("nGr", tab, 1, 1)]
                for qi, (g, tv, ri, half) in enumerate(seq):
                    fw.op("pe", lambda e, g=g, tv=tv, ri=ri, half=half, qi=qi: e.matmul(
                        pK[:, half * K1:(half + 1) * K1], lhsT=H[g], rhs=tv[:, ri, :], start=(qi % 8 == 0), stop=(qi % 8 == 7)),
                        reads=[b1, b2, b3, b4, H["bt"]], writes=[bpK], sig=(qi == 15))
                yield
                fw.op("act", lambda e, c=c: e.activation(out=st[:, c, :], in_=pK[:, 0:2 * K1], func=AF.Copy),
                      reads=[bpK], writes=[bst])
            fw.dma("act", self.KF[o, :, c0:c0 + CBF, 0:2 * K1], st[:], reads=[bst], writes=[self.bKF])
            free.append(slots.index(sl))
            yield

        run_window((group(o, c0) for o in range(2) for c0 in range(0, 512, CBF)), NG)

    def hy_shortconv(self, ph, l, s, L):
        fw, nc, W = self.fw, self.nc, self.W
        cw = self.T(ph, "hcw", [128, 12, 3], F32)
        cb = self.T(ph, "hcb", [128, 12], F32)
        bw = Buf()
        fw.dma("sp", cw[:], W["hy_conv_w"][l], writes=[bw])
        fw.dma("sp", cb[:], W["hy_conv_b"][l], writes=[bw])
        ur = self.ring(ph, "hsu", [128, L + 2], BF16, 2)
        accr = self.ring(ph, "hsa", [128, 2048], F32, 3)
        outr = self.ring(ph, "hso", [128, 2048], BF16, 3)
        PIECE = min(2048, L)
        for q in range(12):
            u, bu = ur.next()
            fw.op("pool", lambda e, u=u: e.memset(u[:, 0:1], 0.0), writes=[bu])
            fw.op("pool", lambda e, u=u: e.memset(u[:, L + 1:L + 2], 0.0), writes=[bu])
            fw.dma("sp", u[:, 1:L + 1], self.AT[q * 128:(q + 1) * 128, 0:L], reads=[self.bAT], writes=[bu])
            for t0 in range(0, L, PIECE):
                a, ba = accr.next()
                o, bo = outr.next()
                fw.op("pool", lambda e, u=u, a=a, q=q, t0=t0: e.tensor_scalar(out=a[:, 0:PIECE], in0=u[:, t0 + 1:t0 + 1 + PIECE],
                                                                             scalar1=cw[:, q, 1:2], scalar2=cb[:, q:q + 1],
                                                                             op0=ALU.mult, op1=ALU.add), reads=[bu, bw], writes=[ba])
                fw.op("dve", lambda e, u=u, a=a, q=q, t0=t0: e.scalar_tensor_tensor(out=a[:, 0:PIECE], in0=u[:, t0:t0 + PIECE],
                                                                                   scalar=cw[:, q, 0:1], in1=a[:, 0:PIECE],
                                                                                   op0=ALU.mult, op1=ALU.add), reads=[bu, bw, ba], writes=[ba])
                fw.op("dve", lambda e, u=u, a=a, o=o, q=q, t0=t0: e.scalar_tensor_tensor(out=o[:, 0:PIECE], in0=u[:, t0 + 2:t0 + 2 + PIECE],
                                                                                        scalar=cw[:, q, 2:3], in1=a[:, 0:PIECE],
                                                                                        op0=ALU.mult, op1=ALU.add), reads=[bu, bw, ba], writes=[bo])
                fw.dma("act", self.UT[q * 128:(q + 1) * 128, t0:t0 + PIECE], o[:, 0:PIECE], reads=[bo], writes=[self.bUT])

    def hy_conv(self, ph, l, s, L):
        fw, nc, H = self.fw, self.nc, self.H
        N, K1, K, nh, kw, C = hy_dims(L)
        CB = 4
        NS = 4
        sel4 = lambda ap, i: ap[:, :, i, :]
        sel5 = lambda ap, i: ap[:, :, :, i, :]

        def stream(st):
            (pA, bpA), (pX, bpX) = [(self.ps[2 * st + i], self.bps[2 * st + i]) for i in range(2)]
            (pB, bpB), (pY, bpY) = (pA, bpA), (pX, bpX)
            gr = self.ring(ph, "hcg%d" % st, [128, 3, CB, 128], BF16, 2)
            yr = self.ring(ph, "hcy%d" % st, [128, CB, 128], BF16, 2)
            kfr = self.ring(ph, "hckf%d" % st, [128, 2, CB, 2 * K1], BF16, 1)
            tar = self.ring(ph, "hcta%d" % st, [128, 512], BF16, 8)
            evr = self.ring(ph, "hcev%d" % st, [128, 512], BF16, 3)
            z1r = self.ring(ph, "hcz1%d" % st, [128, C, 128], BF16, 2)
            g1r = self.ring(ph, "hcg1%d" % st, [128, C * 128], F32, 2)
            g2r = self.ring(ph, "hcg2%d" % st, [128, C * 128], F32, 2)

            def conv(z, bz, xg, bxg, kf, bkf, o, cglob, out, bout):
                for c in range(C):
                    fw.op("pe", lambda e, c=c: e.matmul(pA[:, c * 2 * K1:(c + 1) * 2 * K1], lhsT=z[:, c, :], rhs=H["F1"][:K, :],
                                                        start=True, stop=True), reads=[bz, H["bt"]], writes=[bpA], sig=(c == C - 1))
                yield
                def evt(pat, pat_kw, n, P_):
                    t_, b_ = evr.next()
                    return t_[:P_, 0:n].rearrange("p (" + pat + ") -> p " + pat, **pat_kw), b_

                def prods(shape_pat, **kw_):
                    ta_, bta = tar.next()
                    tb__, btb = tar.next()
                    n_ = 1
                    for v_ in kw_.values():
                        n_ *= v_
                    return ta_, tb__, bta, btb, n_

                ta_, tb__, bta, btb, n_ = prods("c r k", c=C, r=2, k=K1)
                tav = ta_[:, 0:n_].rearrange("p (c r k) -> p c r k", c=C, r=2)
                tbv = tb__[:, 0:n_].rearrange("p (c r k) -> p c r k", c=C, r=2)
                src = pA[:, 0:C * 2 * K1].rearrange("p (c r k) -> p c r k", c=C, r=2)
                bq = Buf()
                self.cprod(src, H["TT"].unsqueeze(1).to_broadcast([128, C, 2, K1]), H["TI"].unsqueeze(1).to_broadcast([128, C, 2, K1]),
                           tav, tbv, [bpA, H["bt"], bta, btb], bq, ev=evt("c r k", pat_kw=dict(c=C, r=2), n=C * 2 * K1, P_=128))
                bta.w = btb.w = bq.w
                yield
                for c in range(C):
                    seq = [("Gr", tav, 0, 0), ("nGr", tbv, 1, 0), ("nGi", tbv, 0, 0), ("nGi", tav, 1, 0),
                           ("Gi", tav, 0, 1), ("nGi", tbv, 1, 1), ("Gr", tbv, 0, 1), ("Gr", tav, 1, 1)]
                    for qi, (g, tv, ri, half) in enumerate(seq):
                        fw.op("pe", lambda e, c=c, g=g, tv=tv, ri=ri, half=half, qi=qi: e.matmul(
                            pX[:, c * 2 * K1 + half * K1:c * 2 * K1 + (half + 1) * K1], lhsT=H[g], rhs=tv[:, c, ri, :],
                            start=(qi % 4 == 0), stop=(qi % 4 == 3)), reads=[bta, btb, H["bt"]], writes=[bpX],
                            sig=(c == C - 1 and qi == 7))
                yield
                ya_, yb_, bya, byb, n_ = prods("c r k", c=C, r=2, k=K1)
                yav = ya_[:, 0:n_].rearrange("p (c r k) -> p c r k", c=C, r=2)
                ybv = yb_[:, 0:n_].rearrange("p (c r k) -> p c r k", c=C, r=2)
                srcx = pX[:, 0:C * 2 * K1].rearrange("p (c r k) -> p c r k", c=C, r=2)
                kfv = kf.rearrange("p c (r k) -> p c r k", r=2)
                bq2 = Buf()
                self.cprod(srcx, kfv[:, :, 0:1, :].to_broadcast([128, C, 2, K1]), kfv[:, :, 1:2, :].to_broadcast([128, C, 2, K1]),
                           yav, ybv, [bpX, bkf, bya, byb], bq2, ev=evt("c r k", pat_kw=dict(c=C, r=2), n=C * 2 * K1, P_=128))
                bya.w = byb.w = bq2.w
                yield
                for c in range(C):
                    for h in range(nh):
                        ob = (c * nh + h) * 256
                        hs = slice(h * kw, (h + 1) * kw)
                        seq = [(yav[:, c, 0, hs], "cG0"), (ybv[:, c, 1, hs], "ncG0"), (ybv[:, c, 0, hs], "cG1"), (yav[:, c, 1, hs], "cG1")]
                        for qi, (lh, g) in enumerate(seq):
                            fw.op("pe", lambda e, lh=lh, g=g, ob=ob, qi=qi: e.matmul(pB[:kw, ob:ob + 256], lhsT=lh, rhs=H[g],
                                                                                   start=(qi == 0), stop=(qi == 3)),
                                  reads=[bya, byb, H["bt"]], writes=[bpB], sig=(c == C - 1 and h == nh - 1 and qi == 3))
                yield
                ba_, bb_, bba, bbb, n_ = prods("h r c m", h=nh, r=2, c=C, m=128)
                bav = ba_[:kw, 0:n_].rearrange("p (h r c m) -> p h r c m", h=nh, r=2, c=C)
                bbv = bb_[:kw, 0:n_].rearrange("p (h r c m) -> p h r c m", h=nh, r=2, c=C)
                srcb = pB[:kw, 0:C * nh * 256].rearrange("p (c h r m) -> p c h r m", c=C, h=nh, r=2)
                bq3 = Buf()
                self.cprod(srcb, H["cTT"][:kw].unsqueeze(1).to_broadcast([kw, C, nh, 2, 128]),
                           H["cTI"][:kw].unsqueeze(1).to_broadcast([kw, C, nh, 2, 128]),
                           bav.rearrange("p h r c m -> p c h r m"), bbv.rearrange("p h r c m -> p c h r m"),
                           [bpB, H["bt"], bba, bbb], bq3, ev=evt("c h r m", pat_kw=dict(c=C, h=nh, r=2), n=C * nh * 256, P_=kw))
                bba.w = bbb.w = bq3.w
                yield
                for h in range(nh):
                    seq = [("cHr", bav, 0), ("ncHr", bbv, 1), ("ncHi", bbv, 0), ("ncHi", bav, 1)]
                    for qi, (g, tv, ri) in enumerate(seq):
                        fw.op("pe", lambda e, h=h, g=g, tv=tv, ri=ri, qi=qi: e.matmul(
                            pY[:K, 0:C * 128], lhsT=H[g][:kw, h, :], rhs=tv[:, h, ri, :, :].rearrange("p c m -> p (c m)"),
                            start=(h == 0 and qi == 0), stop=(h == nh - 1 and qi == 3)),
                            reads=[bba, bbb, H["bt"]], writes=[bpY], sig=(h == nh - 1 and qi == 3))
                yield
                g1, bg1 = g1r.next()
                g2, bg2 = g2r.next()
                g1v = g1[:K, :].rearrange("p (c m) -> p c m", c=C)
                g2v = g2[:K, :].rearrange("p (c m) -> p c m", c=C)
                yv = pY[:K, 0:C * 128].rearrange("p (c m) -> p c m", c=C)
                rinv = H["rinv"][:K, o, cglob:cglob + C].unsqueeze(2).to_broadcast([K, C, 128])
                skp = H["skip"][:K, o, cglob:cglob + C].unsqueeze(2).to_broadcast([K, C, 128])
                fw.op("dve", lambda e: e.tensor_tensor(out=g1v, in0=yv, in1=rinv, op=ALU.mult), reads=[bpY, H["brs"]], writes=[bg1])
                fw.op("pool", lambda e: e.tensor_tensor(out=g2v, in0=z, in1=skp, op=ALU.mult), reads=[bz, H["brs"]], writes=[bg2])
                fw.op("pool", lambda e: e.tensor_tensor(out=g1v, in0=g1v, in1=g2v, op=ALU.add), reads=[bg1, bg2], writes=[bg1])
                fw.op("dve", lambda e: e.tensor_tensor(out=out, in0=g1v, in1=xg, op=ALU.mult), reads=[bg1, bxg], writes=[bout])
                yield

            groups = [g for g in range(512 // CB) if g % NS == st]
            for g in groups:
                c0 = g * CB
                gt, bgt = gr.next()
                for j in range(3):
                    fw.dma("sp", gt[:K, j, :, :], self.UT[j * 512 + c0:j * 512 + c0 + CB, 0:L].rearrange("c (a b) -> a c b", b=128),
                           reads=[self.bUT], writes=[bgt])
                kf, bkf = kfr.next()
                for o in range(2):
                    fw.dma("sp", kf[:, o, :, :], self.KF[o, :, c0:c0 + CB, 0:2 * K1], reads=[self.bKF], writes=[bkf])
                yt_, byt_ = yr.next()
                for cc in range(0, CB, C):
                    z1, bz1 = z1r.next()
                    yield from conv(gt[:K, 0, cc:cc + C, :], bgt, gt[:K, 1, cc:cc + C, :], bgt, kf[:, 0, cc:cc + C, :], bkf, 0,
                                    c0 + cc, z1[:K, :, :], bz1)
                    yield from conv(z1[:K, :, :], bz1, gt[:K, 2, cc:cc + C, :], bgt, kf[:, 1, cc:cc + C, :], bkf, 1,
                                    c0 + cc, yt_[:K, cc:cc + C, :], byt_)
                fw.dma("act", self.YT[c0:c0 + CB, 0:L].rearrange("c (a b) -> a c b", b=128), yt_[:K, :, :], reads=[byt_], writes=[self.bYT])
                yield

        run_window((stream(st) for st in range(NS)), NS)

    Prog.cmul = cmul
    Prog.cprod = cprod
    Prog.hyena = hyena
    Prog.hy_filter = hy_filter
    Prog.hy_filter_fft = hy_filter_fft
    Prog.hy_shortconv = hy_shortconv
    Prog.hy_conv = hy_conv


_hyena_methods()

LS = (8192, 16384)
_CACHE = {}


def _program():
    if "p" not in _CACHE:
        P = Prog(list(LS), depth=DEPTH, debug=False)
        P.build()
        _CACHE["p"] = P
    return _CACHE["p"]


def kernel(x_prompt, x_sample, c_prompt, c_sample, w_ada, b_ada, w_in, b_in, hy_conv_w, hy_conv_b,
           hy_w1, hy_b1, hy_freq, hy_w2, hy_b2, hy_w3, hy_b3, hy_decay, hy_skip, na_rpb,
           w_branch_a, w_branch_b, w_branch_c, w_out, ln_g, ln_b):
    f = lambda a: np.ascontiguousarray(np.asarray(a, dtype=np.float32))
    P = _program()
    raw = dict(w_ada=w_ada, b_ada=b_ada, w_in=w_in, b_in=b_in, hy_conv_w=hy_conv_w, hy_conv_b=hy_conv_b, hy_w1=hy_w1,
               hy_b1=hy_b1, hy_freq=hy_freq, hy_w2=hy_w2, hy_b2=hy_b2, hy_w3=hy_w3, hy_b3=hy_b3, hy_decay=hy_decay,
               hy_skip=hy_skip, w_branch_a=w_branch_a, w_branch_b=w_branch_b, w_branch_c=w_branch_c, w_out=w_out,
               ln_g=ln_g, ln_b=ln_b)
    shared = layout_weights(raw, DEPTH)
    shared["dmask"] = band_masks()
    shared["hyG"] = hyena_g_table()
    rpb = f(na_rpb)
    for s, L in enumerate(LS):
        shared["rope%d" % s] = rope_table(L)
        shared["nab%d" % s] = Prog.na_values(L, rpb)
        shared["hz%d" % s] = hyena_pos(L)
        a32, abf = hyena_tables(L)
        shared["hyf32_%d" % s] = a32
        shared["hybf_%d" % s] = abf
    xp, xs, cp, cs = f(x_prompt), f(x_sample), f(c_prompt), f(c_sample)
    in_maps = []
    for core in range(8):
        m = dict(shared)
        ip, is_ = core % 4, core % 2
        m["x0"], m["c0"] = xp[ip], np.ascontiguousarray(cp[ip].reshape(8, 128).T)
        m["x1"], m["c1"] = xs[is_], np.ascontiguousarray(cs[is_].reshape(8, 128).T)
        in_maps.append(m)
    res = run_bass_kernel_spmd(P.nc, in_maps, core_ids=list(range(8)))
    y_prompt = np.stack([res.results[i]["y0"] for i in range(4)], 0).astype(np.float32)
    y_sample = np.stack([res.results[i]["y1"] for i in range(2)], 0).astype(np.float32)
    return (y_prompt, y_sample)
```

```python
import math
from contextlib import ExitStack
import numpy as np
import ml_dtypes
import concourse.bass as bass
import concourse.mybir as mybir
from concourse.bass_utils import run_bass_kernel_spmd

F32 = mybir.dt.float32
BF16 = mybir.dt.bfloat16
ALU = mybir.AluOpType
AF = mybir.ActivationFunctionType
AX = mybir.AxisListType

D = 1024
DEPTH = 2
D_A = 512
N_IN = 7168
ALPHA = (2 * DEPTH) ** 0.25
GRID_W = 64
NEG = -30000.0


class Buf:
    __slots__ = ("name", "w", "r")

    def __init__(self, name=""):
        self.name = name
        self.w = None
        self.r = []


class FW:
    NDMA = 8
    SEM_LIMIT = 30000
    SAME_ENGINE_SYNC = True

    def __init__(self, nc, ctx):
        self.nc = nc
        self.ctx = ctx
        self.eng = {"pe": nc.tensor, "act": nc.scalar, "dve": nc.vector,
                    "pool": nc.gpsimd, "sp": nc.sync}
        self.sems = {}
        self.cnt = {}
        self.epoch = {}
        for k in ("pe", "act", "dve", "pool"):
            self.epoch[k] = 0
            self._new_epoch_sem(k, 0)
        self.dq = {}
        for q in ("sp", "act", "pool"):
            lst = []
            for i in range(self.NDMA):
                key = "d_%s%d" % (q, i)
                self.sems[key] = ctx.enter_context(nc.semaphore(key))
                self.cnt[key] = 0
                lst.append(key)
            self.dq[q] = [lst, 0]
        self.known = {e: {} for e in self.eng}
        self.nins = 0
        self.same_engine_sync = FW.SAME_ENGINE_SYNC

    def _new_epoch_sem(self, e, ep):
        key = (e, ep)
        self.sems[key] = self.ctx.enter_context(self.nc.semaphore("s_%s_%d" % (e, ep)))
        self.cnt[key] = 0
        return key

    def _next_sig(self, e, commit):
        key = (e, self.epoch[e])
        if self.cnt[key] >= self.SEM_LIMIT:
            if (e, self.epoch[e] + 1) not in self.sems:
                self._new_epoch_sem(e, self.epoch[e] + 1)
            if commit:
                self.epoch[e] += 1
            key = (e, key[1] + 1)
        if commit:
            self.cnt[key] += 1
            return key, self.cnt[key]
        return key, self.cnt[key] + 1

    def _wait(self, e, dep):
        if dep is None:
            return
        key, val = dep
        if isinstance(key, tuple) and key[0] == e and (e == "pe" or not self.same_engine_sync):
            return
        if self.known[e].get(key, 0) >= val:
            return
        self.eng[e].wait_ge(self.sems[key], val)
        self.known[e][key] = val
        self.nins += 1

    def _deps(self, e, reads, writes):
        for b in reads:
            self._wait(e, b.w)
        for b in writes:
            self._wait(e, b.w)
            for d in b.r:
                self._wait(e, d)

    def _commit(self, sig, reads, writes):
        for b in writes:
            b.w = sig
            b.r = []
        for b in reads:
            b.r.append(sig)
            if len(b.r) > 32:
                best = {}
                for k, v in b.r:
                    if best.get(k, 0) < v:
                        best[k] = v
                b.r = list(best.items())

    def op(self, e, fn, reads=(), writes=(), sig=True):
        self._deps(e, reads, writes)
        ins = fn(self.eng[e])
        self.nins += 1
        if sig:
            key, val = self._next_sig(e, True)
            ins.then_inc(self.sems[key], 1)
            s = (key, val)
        else:
            s = self._next_sig(e, False)
        self._commit(s, reads, writes)
        return ins

    def dma(self, q, out, in_, reads=(), writes=(), **kw):
        lst, i = self.dq[q]
        key = lst[i % len(lst)]
        self.dq[q][1] = i + 1
        self._wait(q, (key, self.cnt[key]))
        self._deps(q, reads, writes)
        self.cnt[key] += 16
        ins = self.eng[q].dma_start(out=out, in_=in_, **kw)
        ins.then_inc(self.sems[key], 16)
        self.nins += 1
        self._commit((key, self.cnt[key]), reads, writes)
        return ins

    def drain(self, e="sp"):
        for k in list(self.sems):
            if self.cnt[k] > 0:
                self._wait(e, (k, self.cnt[k]))

    def barrier(self):
        for e in self.eng:
            for k in list(self.sems):
                if self.cnt[k] > 0 and not (isinstance(k, tuple) and k[0] == e):
                    self._wait(e, (k, self.cnt[k]))


class Ring:
    def __init__(self, tiles):
        self.tiles = tiles
        self.bufs = [Buf() for _ in tiles]
        self.i = 0

    def next(self):
        j = self.i % len(self.tiles)
        self.i += 1
        return self.tiles[j], self.bufs[j]


O_A, O_AZ, O_BQKV, O_BZ, O_CQKV, O_CZ, O_G = 0, 1536, 2048, 2816, 3072, 3840, 4096


def _rot_segs(base):
    segs = []
    for h in range(4):
        segs.append((base + h * 64 + 32, 32, -1.0))
        segs.append((base + h * 64, 32, 1.0))
    return segs


def col_plan():
    qs = 0.125
    p = {}
    p["A"] = [(O_A, 1536, 1.0)]
    p["Z"] = [(O_AZ, 512, 1.0), (O_BZ, 256, 1.0), (O_CZ, 256, 1.0)]
    p["BQK"] = [(O_BQKV, 256, qs), (O_BQKV + 256, 256, 1.0)]
    p["CQK"] = ([(O_CQKV, 256, qs)] + [(s, n, m * qs) for (s, n, m) in _rot_segs(O_CQKV)]
                + [(O_CQKV + 256, 256, 1.0)] + _rot_segs(O_CQKV + 256))
    p["V"] = [(O_BQKV + 512, 256, 1.0), (O_CQKV + 512, 256, 1.0)]
    p["G"] = [(O_G, 3072, 1.0)]
    return p


def seg_len(segs):
    return sum(n for _, n, _ in segs)


class Prog:
    def __init__(self, Ls, depth=DEPTH, debug=False):
        self.Ls = Ls
        self.depth = depth
        self.debug = debug
        self.nc = bass.Bass("TRN2", target_bir_lowering=False)
        self.plan = col_plan()
        self.dbg = {}

    def din(self, name, shape, dt=F32):
        return self.nc.dram_tensor(name, list(shape), dt, kind="ExternalInput").ap()

    def dout(self, name, shape, dt=F32):
        return self.nc.dram_tensor(name, list(shape), dt, kind="ExternalOutput").ap()

    def dscr(self, name, shape, dt):
        if self.debug:
            t = self.nc.dram_tensor(name, list(shape), dt, kind="ExternalOutput").ap()
            self.dbg[name] = t
            return t
        return self.nc.dram_tensor(name, list(shape), dt, kind="Internal").ap()

    def T(self, ph, name, shape, dt):
        self._tn = getattr(self, "_tn", 0) + 1
        return ph.enter_context(self.nc.sbuf_tensor("%s_%d" % (name, self._tn), list(shape), dt))

    def ring(self, ph, name, shape, dt, n):
        return Ring([self.T(ph, name, shape, dt) for _ in range(n)])

    def psn(self):
        j = self._psi % 8
        self._psi += 1
        return self.ps[j], self.bps[j]

    def build(self):
        nc = self.nc
        S = len(self.Ls)
        d = self.depth
        W = {}
        W["w_ada"] = self.din("w_ada", [d, D, 3 * D])
        W["b_ada"] = self.din("b_ada", [d, 3 * D])
        W["w_in"] = self.din("w_in", [d, D, N_IN])
        W["b_in"] = self.din("b_in", [d, N_IN])
        W["hy_conv_w"] = self.din("hy_conv_w", [d, 128, 12, 3])
        W["hy_conv_b"] = self.din("hy_conv_b", [d, 128, 12])
        W["hy_w1"] = self.din("hy_w1", [d, 33, 64])
        W["hy_b1"] = self.din("hy_b1", [d, 64, 1])
        W["hy_freq"] = self.din("hy_freq", [d, 64, 2])
        W["hy_w2"] = self.din("hy_w2", [d, 64, 64])
        W["hy_b2"] = self.din("hy_b2", [d, 64, 1])
        W["hy_w3"] = self.din("hy_w3", [d, 64, 2048])
        W["hy_b3"] = self.din("hy_b3", [d, 128, 16])
        W["hy_decay"] = self.din("hy_decay", [d, 128, 4])
        W["hy_skip"] = self.din("hy_skip", [d, 2, 512])
        W["w_branch_a"] = self.din("w_branch_a", [d, 512, D])
        W["w_branch_b"] = self.din("w_branch_b", [d, 256, D])
        W["w_branch_c"] = self.din("w_branch_c", [d, 256, D])
        W["w_out"] = self.din("w_out", [d, D, D])
        W["ln_g"] = self.din("ln_g", [d, D])
        W["ln_b"] = self.din("ln_b", [d, D])
        self.W = W
        self.X = [self.din("x%d" % s, [L, D]) for s, L in enumerate(self.Ls)]
        self.C = [self.din("c%d" % s, [128, 8]) for s in range(S)]
        self.Y = [self.dout("y%d" % s, [L, D]) for s, L in enumerate(self.Ls)]
        self.rope = [self.din("rope%d" % s, [2, 128, L]) for s, L in enumerate(self.Ls)]
        self.nab = [self.din("nab%d" % s, [d] + list(self.na_tables(L)[0].shape[1:]), BF16) for s, L in enumerate(self.Ls)]
        self.dmask = self.din("dmask", [3, 128, 128], BF16)
        self.hz = [self.din("hz%d" % s, [33, L]) for s, L in enumerate(self.Ls)]
        self.hyf32 = []
        self.hybf = []
        for s, L in enumerate(self.Ls):
            a, b = hyena_tables(L)
            self.hyf32.append(self.din("hyf32_%d" % s, list(a.shape), F32))
            self.hybf.append(self.din("hybf_%d" % s, list(b.shape), BF16))
        self.hyG = self.din("hyG", [128, 1280], BF16)
        Lm = max(self.Ls)
        self.XN = [self.dscr("xn%d" % s, [L, D], F32) for s, L in enumerate(self.Ls)]
        self.AT = self.dscr("at", [1536, Lm], BF16)
        self.ZT = self.dscr("zt", [1024, Lm], BF16)
        self.BQK = self.dscr("bqk", [512, Lm], BF16)
        self.CQK = self.dscr("cqk", [512, Lm], BF16)
        self.VT = self.dscr("vt", [Lm, 512], BF16)
        self.GT = self.dscr("gt", [3072, Lm], BF16)
        self.YT = self.din("yt_in", [1024, Lm], BF16) if getattr(self, "yt_in", False) else self.dscr("yt", [1024, Lm], BF16)
        K1m = 2 * Lm // 128
        self.UT = self.dscr("ut", [1536, Lm], BF16)
        self.FT = self.dscr("ft", [2048, Lm], BF16)
        self.KF = self.dscr("kf", [2, 128, 512, 2 * K1m], BF16)
        self.bAT, self.bZT, self.bBQK, self.bCQK, self.bVT, self.bGT, self.bYT = [Buf() for _ in range(7)]
        self.bXN = [Buf() for _ in self.Ls]

        with ExitStack() as ctx:
            self.ctx = ctx
            self.fw = fw = FW(nc, ctx)
            self.ps = [ctx.enter_context(nc.psum_tensor("ps%d" % i, [128, 512], F32)) for i in range(8)]
            self.bps = [Buf() for _ in range(8)]
            self._psi = 0
            self.identf = self.T(ctx, "identf", [128, 128], F32)
            self.ident = self.T(ctx, "ident", [128, 128], BF16)
            self.onesf = self.T(ctx, "onesf", [128, 128], F32)
            self.onesb = self.T(ctx, "onesb", [128, 128], BF16)
            self.epsc = self.T(ctx, "epsc", [128, 1], F32)
            self.bconst = Buf()
            bc = self.bconst
            fw.op("pool", lambda e: e.memset(self.identf[:], 0.0), writes=[bc])
            fw.op("pool", lambda e: e.affine_select(out=self.identf[:], in_=self.identf[:], pattern=[[-1, 128]],
                                                    compare_op=ALU.not_equal, fill=1.0, base=0,
                                                    channel_multiplier=1), reads=[bc], writes=[bc])
            fw.op("pool", lambda e: e.memset(self.onesf[:], 1.0), writes=[bc])
            fw.op("pool", lambda e: e.memset(self.epsc[:], 1e-5), writes=[bc])
            fw.op("dve", lambda e: e.tensor_copy(out=self.ident[:], in_=self.identf[:]), reads=[bc], writes=[bc])
            fw.op("dve", lambda e: e.tensor_copy(out=self.onesb[:], in_=self.onesf[:]), reads=[bc], writes=[bc])
            junk = self.T(ctx, "junk", [1, 64], F32)
            bj = Buf()
            allin = list(W.values()) + self.rope + self.hz + self.X + self.C + self.hyf32
            for ap in allin:
                flat = ap
                while len(flat.shape) > 2:
                    flat = flat[0]
                if len(flat.shape) == 1:
                    flat = flat.rearrange("(a b) -> a b", a=1)
                nj = min(8, flat.shape[1])
                fw.dma("sp", junk[0:1, 0:nj], flat[0:1, 0:nj], writes=[bj])
            junkb = self.T(ctx, "junkb", [1, 64], BF16)
            for ap in self.nab + [self.dmask, self.hyG] + self.hybf + ([self.YT] if getattr(self, "yt_in", False) else []):
                flat = ap
                while len(flat.shape) > 2:
                    flat = flat[0]
                fw.dma("sp", junkb[0:1, 0:8], flat[0:1, 0:8], writes=[bj])
            fw.barrier()
            for l in range(d):
                for s, L in enumerate(self.Ls):
                    xsrc = self.X[s] if l == 0 else self.XN[s]
                    xdst = self.Y[s] if l == d - 1 else self.XN[s]
                    self.bsrc = Buf() if l == 0 else self.bXN[s]
                    self.bdst = Buf() if l == d - 1 else self.bXN[s]
                    self.layer(l, s, L, xsrc, xdst)
            fw.drain("sp")
            fw.barrier()
        return nc

    def layer(self, l, s, L, xsrc, xdst):
        fw = self.fw
        with ExitStack() as lay:
            self.ada(lay, l, s)
            fw.barrier()
            if getattr(self, "stop_after", "") == "ada":
                return
            for groups in (["A", "Z", "BQK", "CQK", "V"], ["G"]):
                with ExitStack() as ph:
                    self.inproj(ph, l, s, L, xsrc, groups)
                fw.barrier()
            if getattr(self, "stop_after", "") == "inproj":
                return
            if not getattr(self, "skip_mixers", False):
                if not getattr(self, "skip_hyena", False):
                    with ExitStack() as ph:
                        self.hyena(ph, l, s, L)
                    fw.barrier()
                with ExitStack() as ph:
                    self.attn(ph, l, s, L)
                fw.barrier()
            with ExitStack() as ph:
                self.merge(ph, l, s, L, xsrc, xdst)
            fw.barrier()

    def ada(self, lay, l, s):
        fw, nc = self.fw, self.nc
        self.shiftc = self.T(lay, "shiftc", [128, 8], F32)
        self.scale1c = self.T(lay, "scale1c", [128, 8], F32)
        self.gatebc = self.T(lay, "gatebc", [128, D], F32)
        self.bada = Buf()
        with ExitStack() as ph:
            cc = self.T(ph, "cc", [128, 8], F32)
            sc = self.T(ph, "sc", [128, 8], F32)
            bcc = Buf()
            war = self.ring(ph, "wa", [128, 3 * D], F32, 2)
            brow = self.T(ph, "brow", [1, 3 * D], F32)
            arow = self.T(ph, "arow", [1, 3 * D], F32)
            barow = Buf()
            fw.dma("sp", cc[:], self.C[s][:, :], writes=[bcc])
            fw.dma("sp", brow[:], self.W["b_ada"][l:l + 1, :], writes=[barow])
            fw.op("act", lambda e: e.activation(out=sc[:], in_=cc[:], func=AF.Silu), reads=[bcc], writes=[bcc])
            for k in range(8):
                wa, bwa = war.next()
                fw.dma("sp", wa[:], self.W["w_ada"][l, k * 128:(k + 1) * 128, :], writes=[bwa])
                for j in range(6):
                    fw.op("pe", lambda e, j=j, k=k, wa=wa: e.matmul(self.ps[j][0:1, :], lhsT=sc[:, k:k + 1],
                                                                     rhs=wa[:, j * 512:(j + 1) * 512],
                                                                     start=(k == 0), stop=(k == 7)),
                          reads=[bcc, bwa], writes=[self.bps[j]], sig=(j == 5))
            for j in range(6):
                fw.op("dve", lambda e, j=j: e.tensor_tensor(out=arow[:, j * 512:(j + 1) * 512], in0=self.ps[j][0:1, :],
                                                            in1=brow[:, j * 512:(j + 1) * 512], op=ALU.add),
                      reads=[self.bps[j], barow], writes=[barow])
            fw.op("dve", lambda e: e.tensor_scalar_add(out=arow[:, D:2 * D], in0=arow[:, D:2 * D], scalar1=1.0),
                  reads=[barow], writes=[barow])
            pc, bpc = self.ps[6], self.bps[6]
            for c in range(16):
                fw.op("pe", lambda e, c=c: e.matmul(pc[:, c:c + 1], lhsT=arow[0:1, c * 128:(c + 1) * 128],
                                                    rhs=self.onesf[0:1, 0:1], start=True, stop=True),
                      reads=[barow, self.bconst], writes=[bpc], sig=(c == 15))
            fw.op("dve", lambda e: e.tensor_copy(out=self.shiftc[:], in_=pc[:, 0:8]), reads=[bpc], writes=[self.bada])
            fw.op("dve", lambda e: e.tensor_copy(out=self.scale1c[:], in_=pc[:, 8:16]), reads=[bpc], writes=[self.bada])
            for h in range(2):
                pg, bpg = self.ps[h], self.bps[h]
                fw.op("pe", lambda e, h=h, pg=pg: e.matmul(pg[:, :], lhsT=self.onesf[0:1, :],
                                                           rhs=arow[0:1, 2 * D + h * 512:2 * D + (h + 1) * 512],
                                                           start=True, stop=True),
                      reads=[barow, self.bconst], writes=[bpg])
                fw.op("act", lambda e, h=h, pg=pg: e.activation(out=self.gatebc[:, h * 512:(h + 1) * 512], in_=pg[:, :],
                                                                func=AF.Copy), reads=[bpg], writes=[self.bada])
            fw.barrier()

    def inproj(self, ph, l, s, L, xsrc, groups):
        fw, nc, plan = self.fw, self.nc, self.plan
        has_v = "V" in groups
        set1 = "A" in groups
        src_lo, src_hi = (0, O_G) if set1 else (O_G, N_IN)
        nsrc = src_hi - src_lo
        segs = []
        gstart = {}
        dst = 0
        for g in groups:
            gstart[g] = dst
            for (src, n, m) in plan[g]:
                segs.append((dst, src, n, m))
                dst += n
        ncols = dst
        nfm = (ncols - (512 if has_v else 0)) // 128
        vstart = gstart.get("V", 0)
        Wp = self.T(ph, "Wp", [128, 8, ncols], BF16)
        bWp = Buf()
        biasc = self.T(ph, "biasc", [128, nfm], F32)
        biasv = self.T(ph, "biasv", [1, 512], BF16)
        bbias = Buf()
        with ExitStack() as p2:
            stg = self.ring(p2, "stg", [128, nsrc], F32, 2)
            binrow = self.T(p2, "binrow", [1, nsrc], F32)
            bsrc = self.T(p2, "bsrc", [1, nsrc], F32)
            browp = self.T(p2, "browp", [1, ncols], F32)
            brow = Buf()
            fw.dma("sp", binrow[:], self.W["b_in"][l:l + 1, src_lo:src_hi], writes=[brow])
            nb = nsrc // 512
            for k in range(8):
                st, bst = stg.next()
                fw.dma("sp", st[:], self.W["w_in"][l, k * 128:(k + 1) * 128, src_lo:src_hi], writes=[bst])
                for i, (dd, src, n, m) in enumerate(segs):
                    fw.op("dve" if i % 2 == 0 else "pool",
                          lambda e, dd=dd, src=src, n=n, m=m, k=k, st=st: e.tensor_scalar(
                              out=Wp[:, k, dd:dd + n], in0=st[:, src - src_lo:src - src_lo + n],
                              scalar1=self.scale1c[:, k:k + 1], scalar2=float(m), op0=ALU.mult, op1=ALU.mult),
                          reads=[bst, self.bada], writes=[bWp])
                for j in range(nb):
                    fw.op("pe", lambda e, j=j, k=k, st=st: e.matmul(self.ps[j][0:1, :], lhsT=self.shiftc[:, k:k + 1],
                                                                     rhs=st[:, j * 512:(j + 1) * 512],
                                                                     start=(k == 0), stop=(k == 7)),
                          reads=[bst, self.bada], writes=[self.bps[j]], sig=(j == nb - 1))
            for j in range(nb):
                fw.op("dve", lambda e, j=j: e.tensor_tensor(out=bsrc[:, j * 512:(j + 1) * 512], in0=self.ps[j][0:1, :],
                                                            in1=binrow[:, j * 512:(j + 1) * 512], op=ALU.add),
                      reads=[self.bps[j], brow], writes=[brow])
            for (dd, src, n, m) in segs:
                fw.op("dve", lambda e, dd=dd, src=src, n=n, m=m: e.tensor_scalar_mul(
                    out=browp[:, dd:dd + n], in0=bsrc[:, src - src_lo:src - src_lo + n], scalar1=float(m)),
                    reads=[brow], writes=[brow])
            pc, bpc = self.ps[0], self.bps[0]
            for c in range(nfm):
                fw.op("pe", lambda e, c=c: e.matmul(pc[:, c:c + 1], lhsT=browp[0:1, c * 128:(c + 1) * 128],
                                                    rhs=self.onesf[0:1, 0:1], start=True, stop=True),
                      reads=[brow, self.bconst], writes=[bpc], sig=(c == nfm - 1))
            fw.op("dve", lambda e: e.tensor_copy(out=biasc[:], in_=pc[:, 0:nfm]), reads=[bpc], writes=[bbias])
            if has_v:
                fw.op("dve", lambda e: e.tensor_copy(out=biasv[:], in_=browp[:, vstart:vstart + 512]),
                      reads=[brow], writes=[bbias])
            fw.barrier()
        chunks = []
        for g in groups:
            if g == "V":
                continue
            c0 = gstart[g] // 128
            n = seg_len(plan[g]) // 128
            for i in range(n):
                chunks.append((g, c0 + i, i))
        dstmap = {"A": (self.AT, self.bAT, AF.Identity), "Z": (self.ZT, self.bZT, AF.Silu),
                  "BQK": (self.BQK, self.bBQK, AF.Identity), "G": (self.GT, self.bGT, AF.Sigmoid)}
        xr = self.ring(ph, "xr", [128, 4, D], F32, 2)
        xnr = self.ring(ph, "xnr", [128, 4, D], BF16, 2)
        hTr = self.ring(ph, "hTr", [128, 8, 512], BF16, 2)
        outr = self.ring(ph, "outr", [128, 512], BF16, 6)
        vor = self.ring(ph, "vor", [128, 512], BF16, 3)
        far = self.ring(ph, "far", [128, 512], F32, 4)
        roper = self.ring(ph, "roper", [128, 2, 512], F32, 2)
        str_ = self.ring(ph, "lnst", [128, 2, 6], F32, 4)
        mvr = self.ring(ph, "lnmv", [128, 2], F32, 4)
        rsr = self.ring(ph, "lnrs", [128, 1], F32, 4)
        nev = 0
        NTB = L // 512
        lnq = {}
        hq = {}

        def prep_ln(tb):
            t0 = tb * 512
            x, bx = xr.next()
            fw.dma("sp", x[:], xsrc[t0:t0 + 512, :].rearrange("(s p) d -> p s d", p=128), reads=[self.bsrc], writes=[bx])
            xn, bxn = xnr.next()
            for sub in range(4):
                self.ln_norm(x[:, sub, :], bx, xn[:, sub, :], bxn, str_, mvr, rsr)
            lnq[tb] = (xn, bxn)

        def prep_T(tb):
            xn, bxn = lnq.pop(tb)
            hT, bhT = hTr.next()
            for sub in range(4):
                pt, bpt = self.psn()
                pT = pt[:].bitcast(BF16)
                for k in range(8):
                    fw.op("pe", lambda e, k=k, sub=sub, pT=pT, xn=xn: e.transpose(
                        out=pT[:, k * 128:(k + 1) * 128], in_=xn[:, sub, k * 128:(k + 1) * 128], identity=self.ident[:]),
                        reads=[bxn, self.bconst], writes=[bpt], sig=(k == 7))
                fw.op("dve" if sub % 2 == 0 else "act",
                      (lambda e, sub=sub, pT=pT, hT=hT: e.tensor_copy(out=hT[:, :, sub * 128:(sub + 1) * 128],
                                                                      in_=pT[:, 0:1024].rearrange("p (k t) -> p k t", k=8)))
                      if sub % 2 == 0 else
                      (lambda e, sub=sub, pT=pT, hT=hT: e.activation(out=hT[:, :, sub * 128:(sub + 1) * 128],
                                                                     in_=pT[:, 0:1024].rearrange("p (k t) -> p k t", k=8),
                                                                     func=AF.Copy)),
                      reads=[bpt], writes=[bhT])
            hq[tb] = (hT, bhT)

        prep_ln(0)
        prep_T(0)
        for tb in range(NTB):
            t0 = tb * 512
            if tb + 1 < NTB:
                prep_ln(tb + 1)
            hT, bhT = hq.pop(tb)
            if "CQK" in groups:
                rp, brp = roper.next()
                fw.dma("sp", rp[:], self.rope[s][:, :, t0:t0 + 512].rearrange("a p t -> p a t"), writes=[brp])
            pend = {}
            for (g, c, i) in chunks:
                pt, bpt = self.psn()
                for k in range(8):
                    fw.op("pe", lambda e, k=k, c=c, pt=pt, hT=hT: e.matmul(pt[:, :], lhsT=Wp[:, k, c * 128:(c + 1) * 128],
                                                                           rhs=hT[:, k, :], start=(k == 0), stop=(k == 7)),
                          reads=[bWp, bhT], writes=[bpt], sig=(k == 7))
                if g == "CQK":
                    fa, bfa = far.next()
                    fw.op("act", lambda e, c=c, pt=pt, fa=fa: e.activation(out=fa[:], in_=pt[:, :], func=AF.Identity,
                                                                          bias=biasc[:, c:c + 1], scale=1.0),
                          reads=[bpt, bbias], writes=[bfa])
                    j, r = divmod(i, 4)
                    hc = r % 2
                    if r < 2:
                        pend[(j, hc)] = (fa, bfa)
                    else:
                        fq, bfq = pend.pop((j, hc))
                        fw.op("dve", lambda e, fq=fq, rp=rp: e.tensor_tensor(out=fq[:], in0=fq[:], in1=rp[:, 0, :], op=ALU.mult),
                              reads=[bfq, brp], writes=[bfq])
                        fw.op("pool", lambda e, fa=fa, rp=rp: e.tensor_tensor(out=fa[:], in0=fa[:], in1=rp[:, 1, :], op=ALU.mult),
                              reads=[bfa, brp], writes=[bfa])
                        o, bo = outr.next()
                        fw.op("dve", lambda e, fq=fq, fa=fa, o=o: e.tensor_tensor(out=o[:], in0=fq[:], in1=fa[:], op=ALU.add),
                              reads=[bfq, bfa], writes=[bo])
                        row = (j * 2 + hc) * 128
                        fw.dma("pool", self.CQK[row:row + 128, t0:t0 + 512], o[:], reads=[bo], writes=[self.bCQK])
                    continue
                dram, bdram, func = dstmap[g]
                o, bo = outr.next()
                if func == AF.Identity and nev % 2 == 0:
                    fw.op("dve", lambda e, c=c, pt=pt, o=o: e.tensor_scalar(out=o[:], in0=pt[:, :], scalar1=biasc[:, c:c + 1],
                                                                            scalar2=None, op0=ALU.add),
                          reads=[bpt, bbias], writes=[bo])
                else:
                    fw.op("act", lambda e, c=c, pt=pt, o=o, func=func: e.activation(out=o[:], in_=pt[:, :], func=func,
                                                                                   bias=biasc[:, c:c + 1], scale=1.0),
                          reads=[bpt, bbias], writes=[bo])
                nev += 1
                fw.dma("pool", dram[i * 128:(i + 1) * 128, t0:t0 + 512], o[:], reads=[bo], writes=[bdram])
            if has_v:
                for sub in range(4):
                    pt, bpt = self.psn()
                    for k in range(8):
                        fw.op("pe", lambda e, k=k, sub=sub, pt=pt, hT=hT: e.matmul(
                            pt[:, :], lhsT=hT[:, k, sub * 128:(sub + 1) * 128], rhs=Wp[:, k, vstart:vstart + 512],
                            start=(k == 0), stop=False), reads=[bWp, bhT], writes=[bpt], sig=False)
                    fw.op("pe", lambda e, pt=pt: e.matmul(pt[:, :], lhsT=self.onesb[0:1, :], rhs=biasv[0:1, :],
                                                          start=False, stop=True),
                          reads=[bbias, self.bconst], writes=[bpt])
                    vo, bvo = vor.next()
                    fw.op("act", lambda e, pt=pt, vo=vo: e.activation(out=vo[:], in_=pt[:, :], func=AF.Copy),
                          reads=[bpt], writes=[bvo])
                    fw.dma("pool", self.VT[t0 + sub * 128:t0 + (sub + 1) * 128, :], vo[:], reads=[bvo], writes=[self.bVT])
            if tb + 1 < NTB:
                prep_T(tb + 1)

    def ln_norm(self, xin, bxin, xout, bxout, str_, mvr, rsr):
        fw = self.fw
        st, bst = str_.next()
        mv, bmv = mvr.next()
        rs, brs = rsr.next()
        for c in range(2):
            fw.op("dve", lambda e, c=c: e.bn_stats(out=st[:, c, :], in_=xin[:, c * 512:(c + 1) * 512]),
                  reads=[bxin], writes=[bst])
        fw.op("dve", lambda e: e.bn_aggr(out=mv[:], in_=st[:]), reads=[bst], writes=[bmv])
        fw.op("act", lambda e: e.activation(out=rs[:], in_=mv[:, 1:2], func=AF.Sqrt, bias=self.epsc[:, 0:1], scale=1.0),
              reads=[bmv, self.bconst], writes=[brs])
        fw.op("dve", lambda e: e.reciprocal(out=rs[:], in_=rs[:]), reads=[brs], writes=[brs])
        fw.op("dve", lambda e: e.tensor_scalar(out=xout, in0=xin, scalar1=mv[:, 0:1], scalar2=rs[:, 0:1],
                                               op0=ALU.subtract, op1=ALU.mult), reads=[bxin, bmv, brs], writes=[bxout])
        return mv, bmv, rs, brs

    def merge(self, ph, l, s, L, xsrc, xdst):
        fw, nc, W = self.fw, self.nc, self.W
        Wbr = self.T(ph, "Wbr", [128, 8, D], BF16)
        Wo = self.T(ph, "Wo", [128, 8, D], BF16)
        lng = self.T(ph, "lng", [128, D], F32)
        lnb = self.T(ph, "lnb", [128, D], F32)
        bW = Buf()
        with ExitStack() as p2:
            stg = self.ring(p2, "mstg", [128, D], F32, 3)
            srcs = [(W["w_branch_a"], 4), (W["w_branch_b"], 2), (W["w_branch_c"], 2)]
            ci = 0
            for (wsrc, n) in srcs:
                for c in range(n):
                    st, bst = stg.next()
                    fw.dma("sp", st[:], wsrc[l, c * 128:(c + 1) * 128, :], writes=[bst])
                    fw.op("act" if ci % 2 else "dve",
                          (lambda e, st=st, ci=ci: e.activation(out=Wbr[:, ci, :], in_=st[:], func=AF.Copy)) if ci % 2 else
                          (lambda e, st=st, ci=ci: e.tensor_copy(out=Wbr[:, ci, :], in_=st[:])),
                          reads=[bst], writes=[bW])
                    ci += 1
            for c in range(8):
                st, bst = stg.next()
                fw.dma("sp", st[:], W["w_out"][l, c * 128:(c + 1) * 128, :], writes=[bst])
                fw.op("act" if c % 2 else "dve",
                      (lambda e, st=st, c=c: e.activation(out=Wo[:, c, :], in_=st[:], func=AF.Copy)) if c % 2 else
                      (lambda e, st=st, c=c: e.tensor_copy(out=Wo[:, c, :], in_=st[:])),
                      reads=[bst], writes=[bW])
            fw.dma("sp", lng[:], W["ln_g"][l:l + 1, :].partition_broadcast(128), writes=[bW])
            fw.dma("sp", lnb[:], W["ln_b"][l:l + 1, :].partition_broadcast(128), writes=[bW])
            fw.barrier()
        yr = self.ring(ph, "my", [128, 8, 512], BF16, 2)
        zr = self.ring(ph, "mz", [128, 8, 512], BF16, 2)
        ygr = self.ring(ph, "myg", [128, 8, 512], BF16, 2)
        gr = self.ring(ph, "mg", [128, 3, 512], BF16, 3)
        mTr = self.ring(ph, "mT", [128, 8, 512], BF16, 2)
        tmr = self.ring(ph, "mtm", [128, 512], F32, 4)
        xr = self.ring(ph, "mx", [128, D], F32, 3)
        rr = self.ring(ph, "mres", [128, D], F32, 3)
        orr = self.ring(ph, "mout", [128, D], F32, 3)
        str_ = self.ring(ph, "mlnst", [128, 2, 6], F32, 4)
        mvr = self.ring(ph, "mlnmv", [128, 2], F32, 4)
        rsr = self.ring(ph, "mlnrs", [128, 1], F32, 4)
        GTv = self.GT.rearrange("(b c p) t -> c p b t", b=3, c=8)
        for tb in range(L // 512):
            t0 = tb * 512
            y, by = yr.next()
            z, bz = zr.next()
            fw.dma("sp", y[:], self.YT[:, t0:t0 + 512].rearrange("(c p) t -> p c t", p=128), reads=[self.bYT], writes=[by])
            fw.dma("sp", z[:], self.ZT[:, t0:t0 + 512].rearrange("(c p) t -> p c t", p=128), reads=[self.bZT], writes=[bz])
            yg, byg = ygr.next()
            fw.op("pool", lambda e, y=y, z=z, yg=yg: e.tensor_tensor(out=yg[:, 0:4, :], in0=y[:, 0:4, :], in1=z[:, 0:4, :], op=ALU.mult),
                  reads=[by, bz], writes=[byg])
            fw.op("dve", lambda e, y=y, z=z, yg=yg: e.tensor_tensor(out=yg[:, 4:8, :], in0=y[:, 4:8, :], in1=z[:, 4:8, :], op=ALU.mult),
                  reads=[by, bz], writes=[byg])
            mT, bmT = mTr.next()
            for dch in range(8):
                g, bg = gr.next()
                fw.dma("sp", g[:], GTv[dch][:, :, t0:t0 + 512], reads=[self.bGT], writes=[bg])
                pts = []
                for (c0, n) in ((0, 4), (4, 2), (6, 2)):
                    pt, bpt = self.psn()
                    for c in range(n):
                        fw.op("pe", lambda e, c=c, c0=c0, n=n, pt=pt, yg=yg, dch=dch: e.matmul(
                            pt[:, :], lhsT=Wbr[:, c0 + c, dch * 128:(dch + 1) * 128], rhs=yg[:, c0 + c, :],
                            start=(c == 0), stop=(c == n - 1)), reads=[bW, byg], writes=[bpt], sig=(c == n - 1))
                    pts.append((pt, bpt))
                t1, bt1 = tmr.next()
                t2, bt2 = tmr.next()
                fw.op("dve", lambda e, t1=t1, g=g, pt=pts[0][0]: e.tensor_tensor(out=t1[:], in0=pt[:, :], in1=g[:, 0, :], op=ALU.mult),
                      reads=[pts[0][1], bg], writes=[bt1])
                fw.op("dve", lambda e, t2=t2, g=g, pt=pts[1][0]: e.tensor_tensor(out=t2[:], in0=pt[:, :], in1=g[:, 1, :], op=ALU.mult),
                      reads=[pts[1][1], bg], writes=[bt2])
                fw.op("pool", lambda e, t1=t1, t2=t2: e.tensor_tensor(out=t1[:], in0=t1[:], in1=t2[:], op=ALU.add),
                      reads=[bt1, bt2], writes=[bt1])
                fw.op("dve", lambda e, t2=t2, g=g, pt=pts[2][0]: e.tensor_tensor(out=t2[:], in0=pt[:, :], in1=g[:, 2, :], op=ALU.mult),
                      reads=[pts[2][1], bg], writes=[bt2])
                fw.op("pool", lambda e, t1=t1, t2=t2, mT=mT, dch=dch: e.tensor_tensor(out=mT[:, dch, :], in0=t1[:], in1=t2[:], op=ALU.add),
                      reads=[bt1, bt2], writes=[bmT])
            for sub in range(4):
                x, bx = xr.next()
                fw.dma("sp", x[:], xsrc[t0 + sub * 128:t0 + (sub + 1) * 128, :], reads=[self.bsrc], writes=[bx])
                res, bres = rr.next()
                for h in range(2):
                    pt, bpt = self.psn()
                    for k in range(8):
                        fw.op("pe", lambda e, k=k, h=h, pt=pt, mT=mT, sub=sub: e.matmul(
                            pt[:, :], lhsT=mT[:, k, sub * 128:(sub + 1) * 128], rhs=Wo[:, k, h * 512:(h + 1) * 512],
                            start=(k == 0), stop=(k == 7)), reads=[bW, bmT], writes=[bpt], sig=(k == 7))
                    fw.op("dve", lambda e, h=h, pt=pt, res=res: e.tensor_tensor(
                        out=res[:, h * 512:(h + 1) * 512], in0=pt[:, :], in1=self.gatebc[:, h * 512:(h + 1) * 512], op=ALU.mult),
                        reads=[bpt, self.bada], writes=[bres])
                fw.op("dve", lambda e, x=x, res=res: e.scalar_tensor_tensor(out=res[:], in0=x[:], scalar=float(ALPHA), in1=res[:],
                                                                            op0=ALU.mult, op1=ALU.add),
                      reads=[bx, bres], writes=[bres])
                o, bo = orr.next()
                self.ln_norm(res[:], bres, o[:], bo, str_, mvr, rsr)
                fw.op("dve", lambda e, o=o: e.tensor_tensor(out=o[:], in0=o[:], in1=lng[:], op=ALU.mult), reads=[bo, bW], writes=[bo])
                fw.op("pool", lambda e, o=o: e.tensor_tensor(out=o[:], in0=o[:], in1=lnb[:], op=ALU.add), reads=[bo, bW], writes=[bo])
                fw.dma("pool", xdst[t0 + sub * 128:t0 + (sub + 1) * 128, :], o[:], reads=[bo], writes=[self.bdst])

    def bXNw(self, s, xdst):
        return self.bXN[s]

    @staticmethod
    def na_tables(L):
        rows = L // GRID_W
        nb = rows // 2
        kq = np.arange(128)
        klr, kc = kq // 64, kq % 64
        qlr, qc = kq // 64, kq % 64
        col_start = np.clip(qc - 8, 0, GRID_W - 16)
        vcol = (kc[:, None] >= col_start[None, :]) & (kc[:, None] < col_start[None, :] + 16)
        dc = np.clip(kc[:, None] - qc[None, :], -15, 15) + 15
        tiles = []
        index = {}
        blocks = []
        for i in range(nb):
            r = 2 * i + qlr
            start = np.clip(r - 4, 0, rows - 8)
            lst = []
            for p in range(max(0, i - 4), min(nb, i + 5)):
                kr = 2 * p + klr
                vrow = (kr[:, None] >= start[None, :]) & (kr[:, None] < start[None, :] + 8)
                valid = vrow & vcol
                if not valid.any():
                    continue
                dr = np.clip(kr[:, None] - r[None, :] + 7, 0, 14)
                key = (p - i, int(start[0] - 2 * i), int(start[64] - 2 * i))
                if key not in index:
                    index[key] = len(tiles)
                    tiles.append((dr, dc, valid))
                lst.append((p, index[key]))
            blocks.append(lst)
        holder = np.zeros((DEPTH, len(tiles), 4, 128, 128), np.float32)
        return holder, blocks, tiles

    @staticmethod
    def na_values(L, rpb):
        holder, blocks, tiles = Prog.na_tables(L)
        out = np.empty(holder.shape, np.float32)
        for l in range(rpb.shape[0]):
            for t, (dr, dc, valid) in enumerate(tiles):
                for h in range(4):
                    out[l, t, h] = np.where(valid, rpb[l, h][dr, dc], np.float32(NEG))
        return out.astype(ml_dtypes.bfloat16)


def rope_table(L):
    half = 32
    inv = (10000.0 ** (-np.arange(half, dtype=np.float32) / half)).astype(np.float32)
    ang = np.arange(L, dtype=np.float32)[None, :] * inv[:, None]
    cos = np.cos(ang).astype(np.float32)
    sin = np.sin(ang).astype(np.float32)
    c = np.concatenate([cos, cos, cos, cos], 0)
    s_ = np.concatenate([sin, sin, sin, sin], 0)
    return np.stack([c, s_], 0)


def band_masks():
    k = np.arange(128)[:, None]
    q = np.arange(128)[None, :]
    out = []
    for off in (-128, 0, 128):
        d = (k + off) - q
        out.append(np.where(np.abs(d) <= 64, 0.0, NEG))
    return np.stack(out, 0).astype(ml_dtypes.bfloat16)


def hyena_pos(L):
    t = (np.arange(L, dtype=np.float32) / np.float32(L)).astype(np.float32)
    bands = np.arange(1, 17, dtype=np.float32)
    ang = (2.0 * math.pi * t[:, None] * bands[None, :]).astype(np.float32)
    z = np.concatenate([t[:, None], np.cos(ang), np.sin(ang)], -1).astype(np.float32)
    return np.ascontiguousarray(z.T)


def run_window(gen_iter, width):
    pending = iter(gen_iter)
    active = []
    done = False
    while True:
        if not done and len(active) < width:
            try:
                active.append(next(pending))
            except StopIteration:
                done = True
        if not active:
            if done:
                return
            continue
        for g in list(active):
            try:
                next(g)
            except StopIteration:
                active.remove(g)


def _attn_methods():
    def attn_block(self, qT, items, rd_bufs, pTr, post):
        fw = self.fw
        n = len(items)
        assert n <= 5
        sbA, bsbA = self.psn()
        pnd, bpn = self.psn()
        pn, pd, bpd = pnd[:, 0:128], pnd[:, 128:256], bpn
        slots = [(sbA[:, j * 128:(j + 1) * 128], bsbA) for j in range(min(n, 4))]
        if n > 4:
            slots.append((pnd[:, 256:384], bpn))
        for i, (kT, v, bias) in enumerate(items):
            st_, bst_ = slots[i]
            fw.op("pe", lambda e, kT=kT, st_=st_: e.matmul(st_, lhsT=kT, rhs=qT, start=True, stop=True),
                  reads=rd_bufs, writes=[bst_])
        yield
        pTs = []
        for i in range(n):
            st_, bst_ = slots[i]
            pT, bpT = pTr.next()
            fw.op("act", lambda e, st_=st_, pT=pT: e.activation(out=pT[:], in_=st_, func=AF.Exp),
                  reads=[bst_], writes=[bpT])
            pTs.append((pT, bpT))
        yield
        for i, (kT, v, bias) in enumerate(items):
            if bias is not None:
                pT, bpT = pTs[i]
                fw.op("dve", lambda e, pT=pT, bias=bias: e.tensor_tensor(out=pT[:], in0=pT[:], in1=bias, op=ALU.mult),
                      reads=rd_bufs + [bpT], writes=[bpT])
        yield
        for i, (kT, v, bias) in enumerate(items):
            pT, bpT = pTs[i]
            fw.op("pe", lambda e, v=v, pT=pT, i=i: e.matmul(pn[0:64, 0:128], lhsT=v, rhs=pT[:], start=(i == 0), stop=(i == n - 1)),
                  reads=rd_bufs + [bpT], writes=[bpn], sig=False)
        for i, (kT, v, bias) in enumerate(items):
            pT, bpT = pTs[i]
            fw.op("pe", lambda e, pT=pT, i=i: e.matmul(pd[0:64, 0:128], lhsT=self.onesb[:, 0:64], rhs=pT[:], start=(i == 0), stop=(i == n - 1)),
                  reads=[bpT, self.bconst], writes=[bpn], sig=(i == n - 1))
        yield
        post(pn, bpn, pd, bpd)

    def attn(self, ph, l, s, L):
        fw, nc = self.fw, self.nc
        nblk = L // 128
        _, na_blocks, na_tiles = self.na_tables(L)
        nt = len(na_tiles)
        KT = self.T(ph, "aKT", [64, L], BF16)
        QT = self.T(ph, "aQT", [64, L], BF16)
        bKQ = Buf()
        pTr = self.ring(ph, "apT", [128, 128], BF16, 24)
        dm = self.T(ph, "adm", [128, 3, 128], BF16)
        bdm = Buf()
        fw.dma("sp", dm[:], self.dmask.rearrange("a k q -> k a q"), writes=[bdm])
        fw.op("act", lambda e: e.activation(out=dm[:], in_=dm[:], func=AF.Exp), reads=[bdm], writes=[bdm])
        with ExitStack() as pb:
            Vh = self.T(pb, "aVh", [128, nblk, 64], BF16)
            nb_ = self.T(pb, "anab", [128, nt, 128], BF16)
            outr = self.ring(pb, "aout", [64, 512], BF16, 3)
            rdr = self.ring(pb, "ard", [64, 128], F32, 6)
            for h in range(4):
                fw.dma("sp", QT[:], self.BQK[h * 64:(h + 1) * 64, 0:L], reads=[self.bBQK], writes=[bKQ])
                fw.dma("sp", KT[:], self.BQK[256 + h * 64:256 + (h + 1) * 64, 0:L], reads=[self.bBQK], writes=[bKQ])
                vsrc = self.VT[0:L, h * 64:(h + 1) * 64].rearrange("(b p) c -> p b c", p=128)
                for b0 in range(0, nblk, 16):
                    b1 = min(nblk, b0 + 16)
                    fw.dma("sp", Vh[:, b0:b1, :], vsrc[:, b0:b1, :], reads=[self.bVT], writes=[bKQ])
                fw.dma("sp", nb_[:], self.nab[s][l, :, h].rearrange("t k q -> k t q"), writes=[bKQ])
                fw.op("act", lambda e: e.activation(out=nb_[:], in_=nb_[:], func=AF.Exp), reads=[bKQ], writes=[bKQ])
                outs = {}

                def na_gens():
                    for i in range(nblk):
                        if i % 4 == 0:
                            outs[i // 4] = outr.next()
                        o, bo = outs[i // 4]
                        items = [(KT[:, p * 128:(p + 1) * 128], Vh[:, p, :], nb_[:, tid, :]) for (p, tid) in na_blocks[i]]

                        def post(pn, bpn, pd, bpd, i=i, o=o, bo=bo):
                            rd, brd = rdr.next()
                            fw.op("dve", lambda e: e.reciprocal(out=rd[:], in_=pd[0:64, 0:128]), reads=[bpd], writes=[brd])
                            fw.op("dve", lambda e: e.tensor_tensor(out=o[:, (i % 4) * 128:(i % 4 + 1) * 128], in0=pn[0:64, 0:128],
                                                                   in1=rd[:], op=ALU.mult), reads=[bpn, bpd, brd], writes=[bo])
                            if i % 4 == 3:
                                fw.dma("pool", self.YT[512 + h * 64:512 + (h + 1) * 64, (i - 3) * 128:(i + 1) * 128], o[:],
                                       reads=[bo], writes=[self.bYT])
                        yield self.attn_block(QT[:, i * 128:(i + 1) * 128], items, [bKQ], pTr, post)
                run_window(na_gens(), 4)
            fw.barrier()
        with ExitStack() as pc:
            RNG = min(2048, L)
            pats = (1, 4, 16)
            Vd = {d: self.T(pc, "aVd%d" % d, [128, d, L // (128 * d), 64], BF16) for d in pats}
            accn = self.T(pc, "accn", [64, RNG], F32)
            accd = self.T(pc, "accd", [64, RNG], F32)
            bacc = Buf()
            yo = self.ring(pc, "ayo", [64, RNG], BF16, 2)
            for h in range(4):
                fw.dma("sp", QT[:], self.CQK[h * 64:(h + 1) * 64, 0:L], reads=[self.bCQK], writes=[bKQ])
                fw.dma("sp", KT[:], self.CQK[256 + h * 64:256 + (h + 1) * 64, 0:L], reads=[self.bCQK], writes=[bKQ])
                for d in pats:
                    src = self.VT[0:L, 256 + h * 64:256 + (h + 1) * 64].rearrange("(m dd) c -> dd m c", dd=d)
                    for j in range(d):
                        vs = src[j].rearrange("(b p) c -> p b c", p=128)
                        nbd = L // (128 * d)
                        for b0 in range(0, nbd, 16):
                            b1 = min(nbd, b0 + 16)
                            fw.dma("sp", Vd[d][:, j, b0:b1, :], vs[:, b0:b1, :], reads=[self.bVT], writes=[bKQ])
                for r0 in range(0, L, RNG):
                    def dil_gens(r0=r0):
                        for d in pats:
                            KTv = KT[:].rearrange("p (m dd) -> p dd m", dd=d)
                            QTv = QT[:].rearrange("p (m dd) -> p dd m", dd=d)
                            nkb = L // (128 * d)
                            av_n = accn[:].rearrange("p (m dd) -> p dd m", dd=d)
                            av_d = accd[:].rearrange("p (m dd) -> p dd m", dd=d)
                            for j in range(d):
                                for qb in range(RNG // (128 * d)):
                                    qbg = r0 // (128 * d) + qb
                                    items = []
                                    for mi, kb in enumerate((qbg - 1, qbg, qbg + 1)):
                                        if kb < 0 or kb >= nkb:
                                            continue
                                        items.append((KTv[:, j, kb * 128:(kb + 1) * 128], Vd[d][:, j, kb, :], dm[:, mi, :]))
                                    on = av_n[:, j, qb * 128:(qb + 1) * 128]
                                    od = av_d[:, j, qb * 128:(qb + 1) * 128]

                                    def post(pn, bpn, pd, bpd, on=on, od=od, d=d):
                                        if d == 1:
                                            fw.op("dve", lambda e: e.tensor_copy(out=on, in_=pn[0:64, 0:128]), reads=[bpn, bpd], writes=[bacc])
                                            fw.op("dve", lambda e: e.tensor_copy(out=od, in_=pd[0:64, 0:128]), reads=[bpd], writes=[bacc])
                                        else:
                                            fw.op("dve", lambda e: e.tensor_tensor(out=on, in0=pn[0:64, 0:128], in1=on, op=ALU.add),
                                                  reads=[bpn, bpd, bacc], writes=[bacc])
                                            fw.op("dve", lambda e: e.tensor_tensor(out=od, in0=pd[0:64, 0:128], in1=od, op=ALU.add),
                                                  reads=[bpd, bacc], writes=[bacc])
                                    yield self.attn_block(QTv[:, j, qbg * 128:(qbg + 1) * 128], items, [bKQ, bdm], pTr, post)
                    run_window(dil_gens(), 4)
                    y, by = yo.next()
                    fw.op("dve", lambda e: e.reciprocal(out=accd[:], in_=accd[:]), reads=[bacc], writes=[bacc])
                    fw.op("dve", lambda e, y=y: e.tensor_tensor(out=y[:], in0=accn[:], in1=accd[:], op=ALU.mult),
                          reads=[bacc], writes=[by])
                    fw.dma("pool", self.YT[768 + h * 64:768 + (h + 1) * 64, r0:r0 + RNG], y[:], reads=[by], writes=[self.bYT])
            fw.barrier()

    Prog.attn_block = attn_block
    Prog.attn = attn


_attn_methods()


TWO_PI = 2.0 * math.pi
MAGIC = 12582912.0


def hy_dims(L):
    N = 2 * L
    K1 = N // 128
    K = L // 128
    nh = max(1, K1 // 128)
    kw = min(K1, 128)
    C = max(1, min(2, 512 // (2 * K1)))
    return N, K1, K, nh, kw, C


def hyena_tables(L):
    N, K1, K, nh, kw, C = hy_dims(L)
    n1 = np.arange(128)[:, None].astype(np.float64)
    k1 = np.arange(K1)[None, :].astype(np.float64)
    F1 = np.zeros((128, 2 * K1), np.float64)
    ang = 2 * np.pi * n1 * k1 / K1
    F1[:, :K1] = np.cos(ang)
    F1[:, K1:] = -np.sin(ang)
    F1[K:, :] = 0.0
    angT = 2 * np.pi * n1 * k1 / N
    Tr, Ti = np.cos(angT), -np.sin(angT)
    TT = np.stack([Tr, Tr], 1)
    TI = np.stack([Ti, Ti], 1)
    p = np.arange(128)[:, None, None].astype(np.float64)
    h = np.arange(nh)[None, :, None].astype(np.float64)
    m2 = np.arange(128)[None, None, :].astype(np.float64)
    kk = h * kw + p
    angc = 2 * np.pi * m2 * kk / N
    cTr, cTi = np.cos(angc), np.sin(angc)
    cTT = np.stack([cTr, cTr], 2)
    cTI = np.stack([cTi, cTi], 2)
    m1 = np.arange(K)[None, None, :].astype(np.float64)
    angh = 2 * np.pi * m1 * kk / K1
    cHr = np.cos(angh) / N
    ncHi = -np.sin(angh) / N
    valid = (np.arange(128) < kw)[:, None, None]
    cHr, ncHi = cHr * valid, ncHi * valid
    f32 = np.concatenate([TT.reshape(128, -1), TI.reshape(128, -1), cTT.reshape(128, -1), cTI.reshape(128, -1)], 1)
    bf = np.concatenate([F1, cHr.reshape(128, -1), ncHi.reshape(128, -1), -cHr.reshape(128, -1)], 1)
    return f32.astype(np.float32), bf.astype(ml_dtypes.bfloat16)


def hyena_g_table():
    a = np.arange(128)[:, None].astype(np.float64)
    b = np.arange(128)[None, :].astype(np.float64)
    ang = 2 * np.pi * a * b / 128
    Gr, Gi = np.cos(ang), -np.sin(ang)
    G4 = np.stack([Gr, Gi, -Gi, -Gr], 1).reshape(128, 512)
    cG = np.concatenate([Gr, -Gi, Gi, Gr, -Gr, Gi], 1)
    return np.concatenate([G4, cG], 1).astype(ml_dtypes.bfloat16)


def layout_weights(raw, depth):
    f = lambda a: np.ascontiguousarray(np.asarray(a, dtype=np.float32)[:depth])
    out = {k: f(raw[k]) for k in ("w_ada", "b_ada", "w_in", "b_in", "hy_w1", "hy_w2", "hy_w3", "hy_skip",
                                  "w_branch_a", "w_branch_b", "w_branch_c", "w_out", "ln_g", "ln_b")}
    d = depth
    out["hy_conv_w"] = np.ascontiguousarray(f(raw["hy_conv_w"]).reshape(d, 3, 12, 128).transpose(0, 3, 2, 1))
    out["hy_conv_b"] = np.ascontiguousarray(f(raw["hy_conv_b"]).reshape(d, 12, 128).transpose(0, 2, 1))
    out["hy_b3"] = np.ascontiguousarray(f(raw["hy_b3"]).reshape(d, 16, 128).transpose(0, 2, 1))
    out["hy_decay"] = np.ascontiguousarray(f(raw["hy_decay"]).reshape(d, 4, 128).transpose(0, 2, 1))
    out["hy_b1"] = f(raw["hy_b1"]).reshape(d, 64, 1)
    out["hy_b2"] = f(raw["hy_b2"]).reshape(d, 64, 1)
    out["hy_freq"] = np.ascontiguousarray(f(raw["hy_freq"]).transpose(0, 2, 1))
    return out


def _hyena_methods():
    def cmul(self, src, tR, tI, outRe, outIm, sel, shape, tmpr, rd, wr, bwr):
        fw = self.fw
        n = 1
        for v in shape[1:]:
            n *= v
        ta_, bta = tmpr.next()
        tb_, btb = tmpr.next()
        names = "abcdefg"[:len(shape) - 1]
        pat = "p (" + " ".join(names) + ") -> p " + " ".join(names)
        kw_ = {nm: v for nm, v in zip(names, shape[1:])}
        P_ = shape[0]
        ta = ta_[:P_, 0:n].rearrange(pat, **kw_)
        tb = tb_[:P_, 0:n].rearrange(pat, **kw_)
        fw.op("dve", lambda e: e.tensor_tensor(out=ta, in0=src, in1=tR, op=ALU.mult), reads=rd, writes=[bta])
        fw.op("dve", lambda e: e.tensor_tensor(out=tb, in0=src, in1=tI, op=ALU.mult), reads=rd, writes=[btb])
        fw.op("pool", lambda e: e.tensor_tensor(out=outRe, in0=sel(ta, 0), in1=sel(tb, 1), op=ALU.subtract),
              reads=[bta, btb], writes=[bwr])
        fw.op("pool", lambda e: e.tensor_tensor(out=outIm, in0=sel(tb, 0), in1=sel(ta, 1), op=ALU.add),
              reads=[bta, btb], writes=[bwr])

    def cprod(self, src, tR, tI, ta, tb, rd, bta, ev=None):
        fw = self.fw
        if ev is not None:
            evt, bev = ev
            fw.op("act", lambda e: e.activation(out=evt, in_=src, func=AF.Copy), reads=rd, writes=[bev])
            src = evt
            rd = list(rd) + [bev]
        fw.op("dve", lambda e: e.tensor_tensor(out=ta, in0=src, in1=tR, op=ALU.mult), reads=rd, writes=[bta])
        fw.op("dve", lambda e: e.tensor_tensor(out=tb, in0=src, in1=tI, op=ALU.mult), reads=rd, writes=[bta])

    def hyena(self, ph, l, s, L):
        fw, nc, W = self.fw, self.nc, self.W
        N, K1, K, nh, kw, C = hy_dims(L)
        nf32 = 2 * 2 * K1 + 2 * nh * 2 * 128
        tf = self.T(ph, "hytf", [128, nf32], F32)
        nbf = 2 * K1 + 3 * nh * K
        tb_ = self.T(ph, "hytb", [128, nbf], BF16)
        tg = self.T(ph, "hytg", [128, 1280], BF16)
        bt = Buf()
        fw.dma("sp", tf[:], self.hyf32[s][:, :], writes=[bt])
        fw.dma("sp", tb_[:], self.hybf[s][:, :], writes=[bt])
        fw.dma("sp", tg[:], self.hyG[:, :], writes=[bt])
        H = {"bt": bt}
        tfb = self.T(ph, "hytfb", [128, nf32], BF16)
        fw.op("dve", lambda e: e.tensor_copy(out=tfb[:], in_=tf[:]), reads=[bt], writes=[bt])
        H["TT"] = tfb[:, 0:2 * K1].rearrange("p (r k) -> p r k", r=2)
        H["TI"] = tfb[:, 2 * K1:4 * K1].rearrange("p (r k) -> p r k", r=2)
        o_ = 4 * K1
        H["cTT"] = tfb[:, o_:o_ + nh * 256].rearrange("p (h r m) -> p h r m", h=nh, r=2)
        H["cTI"] = tfb[:, o_ + nh * 256:o_ + 2 * nh * 256].rearrange("p (h r m) -> p h r m", h=nh, r=2)
        H["F1"] = tb_[:, 0:2 * K1]
        H["cHr"] = tb_[:, 2 * K1:2 * K1 + nh * K].rearrange("p (h m) -> p h m", h=nh)
        H["ncHi"] = tb_[:, 2 * K1 + nh * K:2 * K1 + 2 * nh * K].rearrange("p (h m) -> p h m", h=nh)
        H["Gr"], H["Gi"], H["nGi"], H["nGr"] = [tg[:, i * 128:(i + 1) * 128] for i in range(4)]
        H["cG0"], H["cG1"], H["ncG0"] = tg[:, 512:768], tg[:, 768:1024], tg[:, 1024:1280]
        H["ncHr"] = tb_[:, 2 * K1 + 2 * nh * K:2 * K1 + 3 * nh * K].rearrange("p (h m) -> p h m", h=nh)
        self.H = H
        rinv_bc = self.T(ph, "hyrinv", [128, 2, 512], F32)
        skip_bc = self.T(ph, "hyskip", [128, 2, 512], F32)
        H["rinv"], H["skip"], H["brs"] = rinv_bc, skip_bc, Buf()
        fw.dma("sp", skip_bc[:].rearrange("p o c -> p (o c)"),
               W["hy_skip"][l:l + 1].rearrange("a o c -> a (o c)").partition_broadcast(128), writes=[H["brs"]])
        self.bUT, self.bFT, self.bKF = Buf(), Buf(), Buf()
        with ExitStack() as p2:
            self.hy_filter(p2, l, s, L)
        fw.barrier()
        with ExitStack() as p2:
            self.hy_filter_fft(p2, l, s, L)
        fw.barrier()
        with ExitStack() as p2:
            self.hy_shortconv(p2, l, s, L)
        fw.barrier()
        with ExitStack() as p2:
            self.hy_conv(p2, l, s, L)
        fw.barrier()

    def hy_filter(self, ph, l, s, L):
        fw, nc, W, H = self.fw, self.nc, self.W, self.H
        NT = L // 512
        w1 = self.T(ph, "hw1", [33, 64], F32)
        w2 = self.T(ph, "hw2", [64, 64], F32)
        w3 = self.T(ph, "hw3", [64, 2048], F32)
        b1c = self.T(ph, "hb1", [64, 1], F32)
        b2c = self.T(ph, "hb2", [64, 1], F32)
        fq = self.T(ph, "hfq", [64, 2], F32)
        b3c = self.T(ph, "hb3", [128, 16], F32)
        dec = self.T(ph, "hdec", [128, 4], F32)
        cols = self.T(ph, "hcols", [64, 4], F32)
        stats = self.T(ph, "hstats", [128, 16, NT], F32)
        bw = Buf()
        bstat = Buf()
        fw.dma("sp", w1[:], W["hy_w1"][l], writes=[bw])
        fw.dma("sp", w2[:], W["hy_w2"][l], writes=[bw])
        fw.dma("sp", w3[:], W["hy_w3"][l], writes=[bw])
        fw.dma("sp", b1c[:], W["hy_b1"][l], writes=[bw])
        fw.dma("sp", b2c[:], W["hy_b2"][l], writes=[bw])
        fw.dma("sp", fq[:], W["hy_freq"][l], writes=[bw])
        fw.dma("sp", b3c[:], W["hy_b3"][l], writes=[bw])
        fw.dma("sp", dec[:], W["hy_decay"][l], writes=[bw])
        fw.op("dve", lambda e: e.tensor_scalar_mul(out=cols[:, 0:1], in0=fq[:, 0:1], scalar1=1.0 / TWO_PI), reads=[bw], writes=[bw])
        fw.op("dve", lambda e: e.tensor_tensor(out=cols[:, 1:2], in0=cols[:, 0:1], in1=b1c[:], op=ALU.mult), reads=[bw], writes=[bw])
        fw.op("dve", lambda e: e.tensor_scalar_mul(out=cols[:, 2:3], in0=fq[:, 1:2], scalar1=1.0 / TWO_PI), reads=[bw], writes=[bw])
        fw.op("dve", lambda e: e.tensor_tensor(out=cols[:, 3:4], in0=cols[:, 2:3], in1=b2c[:], op=ALU.mult), reads=[bw], writes=[bw])
        ndec = self.T(ph, "hndec", [128, 4], F32)
        fw.op("dve", lambda e: e.tensor_scalar_mul(out=ndec[:], in0=dec[:], scalar1=-1.0), reads=[bw], writes=[bw])
        fw.op("dve", lambda e: e.tensor_tensor(out=dec[:], in0=dec[:], in1=ndec[:], op=ALU.min), reads=[bw], writes=[bw])
        fw.op("pool", lambda e: e.memset(stats[:], 0.0), writes=[bstat])
        ztr = self.ring(ph, "hzt", [33, 512], F32, 2)
        tbr = self.ring(ph, "htb", [128, 512], F32, 2)
        ur = self.ring(ph, "hu", [64, 512], F32, 3)
        tr_ = self.ring(ph, "ht", [64, 512], F32, 3)
        hr = self.ring(ph, "hh", [64, 512], F32, 4)
        er = self.ring(ph, "hE", [128, 512], F32, 8)
        fr = self.ring(ph, "hf", [128, 512], BF16, 4)

        def sin_layer(pre, bpre, ca, cb):
            u, bu = ur.next()
            t, btt = tr_.next()
            h, bh = hr.next()
            fw.op("dve", lambda e: e.tensor_scalar(out=u[:], in0=pre, scalar1=cols[:, ca:ca + 1], scalar2=cols[:, cb:cb + 1],
                                                   op0=ALU.mult, op1=ALU.add), reads=[bpre, bw], writes=[bu])
            fw.op("pool", lambda e: e.tensor_scalar_add(out=t[:], in0=u[:], scalar1=MAGIC), reads=[bu], writes=[btt])
            fw.op("dve", lambda e: e.scalar_tensor_tensor(out=t[:], in0=t[:], scalar=-MAGIC, in1=u[:], op0=ALU.add, op1=ALU.subtract),
                  reads=[btt, bu], writes=[btt])
            fw.op("act", lambda e: e.activation(out=h[:], in_=t[:], func=AF.Sin, scale=-TWO_PI), reads=[btt], writes=[bh])
            return h, bh

        for ti in range(NT):
            t0 = ti * 512
            zt, bzt = ztr.next()
            tbt, btb = tbr.next()
            fw.dma("sp", zt[:], self.hz[s][:, t0:t0 + 512], writes=[bzt])
            fw.dma("sp", tbt[:], self.hz[s][0:1, t0:t0 + 512].partition_broadcast(128), writes=[btb])
            p1, bp1 = self.psn()
            fw.op("pe", lambda e: e.matmul(p1[0:64, :], lhsT=w1[:, :], rhs=zt[:, :], start=True, stop=True),
                  reads=[bw, bzt], writes=[bp1])
            h1, bh1 = sin_layer(p1[0:64, :], bp1, 0, 1)
            p2_, bp2 = self.psn()
            fw.op("pe", lambda e: e.matmul(p2_[0:64, :], lhsT=w2[:, :], rhs=h1[:, :], start=True, stop=True),
                  reads=[bw, bh1], writes=[bp2])
            h2, bh2 = sin_layer(p2_[0:64, :], bp2, 2, 3)
            Es = []
            for q in range(4):
                E, bE = er.next()
                fw.op("act", lambda e, E=E, q=q: e.activation(out=E[:], in_=tbt[:], func=AF.Exp, scale=dec[:, q:q + 1]),
                      reads=[btb, bw], writes=[bE])
                Es.append((E, bE))
            for q16 in range(16):
                p3, bp3 = self.psn()
                fw.op("pe", lambda e, q16=q16, p3=p3: e.matmul(p3[:, :], lhsT=w3[:, q16 * 128:(q16 + 1) * 128], rhs=h2[:, :],
                                                               start=True, stop=True), reads=[bw, bh2], writes=[bp3])
                E, bE = Es[q16 % 4]
                ft, bft = fr.next()
                fw.op("dve", lambda e, q16=q16, p3=p3, E=E, ft=ft: e.scalar_tensor_tensor(
                    out=ft[:], in0=p3[:, :], scalar=b3c[:, q16:q16 + 1], in1=E[:], op0=ALU.add, op1=ALU.mult),
                    reads=[bp3, bE, bw], writes=[bft])
                if q16 >= 8 and ti == 0:
                    fw.op("pool", lambda e, ft=ft: e.memset(ft[:, 0:1], 0.0), reads=[bft], writes=[bft])
                fw.op("dve", lambda e, q16=q16, ft=ft, ti=ti: e.tensor_reduce(out=stats[:, q16, ti:ti + 1], in_=ft[:], axis=AX.X,
                                                                              op=ALU.add, apply_absolute_value=True),
                      reads=[bft], writes=[bstat])
                fw.dma("act", self.FT[q16 * 128:(q16 + 1) * 128, t0:t0 + 512], ft[:], reads=[bft], writes=[self.bFT])
        S = self.T(ph, "hS", [128, 16], F32)
        Dm = self.T(ph, "hD", [128, 8, 128], F32)
        fw.op("dve", lambda e: e.tensor_reduce(out=S[:], in_=stats[:], axis=AX.X, op=ALU.add), reads=[bstat], writes=[bstat])
        fw.op("dve", lambda e: e.tensor_tensor(out=S[:, 0:8], in0=S[:, 0:8], in1=S[:, 8:16], op=ALU.add), reads=[bstat], writes=[bstat])
        fw.op("dve", lambda e: e.tensor_scalar_add(out=S[:, 0:8], in0=S[:, 0:8], scalar1=1e-6), reads=[bstat], writes=[bstat])
        fw.op("dve", lambda e: e.reciprocal(out=S[:, 0:8], in_=S[:, 0:8]), reads=[bstat], writes=[bstat])
        for j in range(8):
            fw.op("dve", lambda e, j=j: e.tensor_scalar_mul(out=Dm[:, j, :], in0=self.identf[:], scalar1=S[:, j:j + 1]),
                  reads=[bstat, self.bconst], writes=[bstat])
        for o in range(2):
            pt, bpt = self.psn()
            fw.op("pe", lambda e, o=o, pt=pt: e.matmul(pt[:, :], lhsT=self.onesf[:, :],
                                                       rhs=Dm[:, 4 * o:4 * o + 4, :].rearrange("p a b -> p (a b)"),
                                                       start=True, stop=True), reads=[bstat, self.bconst], writes=[bpt])
            fw.op("act", lambda e, o=o, pt=pt: e.activation(out=H["rinv"][:, o, :], in_=pt[:, :], func=AF.Copy),
                  reads=[bpt], writes=[H["brs"]])

    def hy_filter_fft(self, ph, l, s, L):
        fw, nc, H = self.fw, self.nc, self.H
        N, K1, K, nh, kw, C = hy_dims(L)
        CBF = 8
        NG = 4
        sel4 = lambda ap, i: ap[:, :, i, :]
        slots = []
        for g in range(NG):
            slots.append(dict(fg=self.ring(ph, "hfg%d" % g, [128, 2, CBF, 128], BF16, 1),
                              stg=self.ring(ph, "hstg%d" % g, [128, CBF, 2 * K1], BF16, 1),
                              tmpr=self.ring(ph, "hftmp%d" % g, [128, 512], BF16, 12),
                              banks=[(self.ps[2 * g], self.bps[2 * g]), (self.ps[2 * g + 1], self.bps[2 * g + 1])]))
        free = list(range(NG))

        def group(o, c0):
            sl = slots[free.pop(0)]
            fg, bfg = sl["fg"].next()
            for dr in range(2):
                r0 = dr * 1024 + o * 512 + c0
                fw.dma("sp", fg[:K, dr, :, :], self.FT[r0:r0 + CBF, 0:L].rearrange("c (a b) -> a c b", b=128),
                       reads=[self.bFT], writes=[bfg])
            st, bst = sl["stg"].next()
            yield
            for c in range(CBF):
                for dr in range(2):
                    pA, bpA = sl["banks"][dr]
                    fw.op("pe", lambda e, pA=pA, dr=dr, c=c: e.matmul(pA[:, 0:2 * K1], lhsT=fg[:K, dr, c, :], rhs=H["F1"][:K, :],
                                                                       start=True, stop=True),
                          reads=[bfg, H["bt"]], writes=[bpA])
                yield
                prods = []
                bpr_ = Buf()
                for dr in range(2):
                    pA, bpA = sl["banks"][dr]
                    ta_, _b1 = sl["tmpr"].next()
                    tb__, _b2 = sl["tmpr"].next()
                    src = pA[:, 0:2 * K1].rearrange("p (r k) -> p r k", r=2)
                    tav = ta_[:, 0:2 * K1].rearrange("p (r k) -> p r k", r=2)
                    tbv = tb__[:, 0:2 * K1].rearrange("p (r k) -> p r k", r=2)
                    ev_, _b3 = sl["tmpr"].next()
                    self.cprod(src, H["TT"], H["TI"], tav, tbv, [bpA, H["bt"], _b1, _b2], bpr_,
                               ev=(ev_[:, 0:2 * K1].rearrange("p (r k) -> p r k", r=2), _b3))
                    _b1.w = _b2.w = bpr_.w
                    prods.append((tav, tbv, _b1, _b2))
                yield
                pK, bpK = sl["banks"][0]
                (taf, tbf, b1, b2), (tab, tbb, b3, b4) = prods
                seq = [("Gr", taf, 0, 0), ("Gr", tab, 0, 0), ("nGr", tbf, 1, 0), ("nGr", tbb, 1, 0),
                       ("nGi", tbf, 0, 0), ("nGi", taf, 1, 0), ("nGi", tbb, 0, 0), ("nGi", tab, 1, 0),
                       ("Gi", taf, 0, 1), ("nGi", tbf, 1, 1), ("Gr", tbf, 0, 1), ("Gr", taf, 1, 1),
                       ("nGi", tab, 0, 1), ("Gi", tbb, 1, 1), ("nGr", tbb, 0, 1), ("nGr", tab, 1, 1)]
                for qi, (g, tv, ri, half) in enumerate(seq):
                    fw.op("pe", lambda e, g=g, tv=tv, ri=ri, half=half, qi=qi: e.matmul(
                        pK[:, half * K1:(half + 1) * K1], lhsT=H[g], rhs=tv[:, ri, :], start=(qi % 8 == 0), stop=(qi % 8 == 7)),
                        reads=[b1, b2, b3, b4, H["bt"]], writes=[bpK], sig=(qi == 15))
                yield
                fw.op("act", lambda e, c=c: e.activation(out=st[:, c, :], in_=pK[:, 0:2 * K1], func=AF.Copy),
                      reads=[bpK], writes=[bst])
            fw.dma("act", self.KF[o, :, c0:c0 + CBF, 0:2 * K1], st[:], reads=[bst], writes=[self.bKF])
            free.append(slots.index(sl))
            yield

        run_window((group(o, c0) for o in range(2) for c0 in range(0, 512, CBF)), NG)

    def hy_shortconv(self, ph, l, s, L):
        fw, nc, W = self.fw, self.nc, self.W
        cw = self.T(ph, "hcw", [128, 12, 3], F32)
        cb = self.T(ph, "hcb", [128, 12], F32)
        bw = Buf()
        fw.dma("sp", cw[:], W["hy_conv_w"][l], writes=[bw])
        fw.dma("sp", cb[:], W["hy_conv_b"][l], writes=[bw])
        ur = self.ring(ph, "hsu", [128, L + 2], BF16, 2)
        accr = self.ring(ph, "hsa", [128, 2048], F32, 3)
        outr = self.ring(ph, "hso", [128, 2048], BF16, 3)
        PIECE = min(2048, L)
        for q in range(12):
            u, bu = ur.next()
            fw.op("pool", lambda e, u=u: e.memset(u[:, 0:1], 0.0), writes=[bu])
            fw.op("pool", lambda e, u=u: e.memset(u[:, L + 1:L + 2], 0.0), writes=[bu])
            fw.dma("sp", u[:, 1:L + 1], self.AT[q * 128:(q + 1) * 128, 0:L], reads=[self.bAT], writes=[bu])
            for t0 in range(0, L, PIECE):
                a, ba = accr.next()
                o, bo = outr.next()
                fw.op("pool", lambda e, u=u, a=a, q=q, t0=t0: e.tensor_scalar(out=a[:, 0:PIECE], in0=u[:, t0 + 1:t0 + 1 + PIECE],
                                                                             scalar1=cw[:, q, 1:2], scalar2=cb[:, q:q + 1],
                                                                             op0=ALU.mult, op1=ALU.add), reads=[bu, bw], writes=[ba])
                fw.op("dve", lambda e, u=u, a=a, q=q, t0=t0: e.scalar_tensor_tensor(out=a[:, 0:PIECE], in0=u[:, t0:t0 + PIECE],
                                                                                   scalar=cw[:, q, 0:1], in1=a[:, 0:PIECE],
                                                                                   op0=ALU.mult, op1=ALU.add), reads=[bu, bw, ba], writes=[ba])
                fw.op("dve", lambda e, u=u, a=a, o=o, q=q, t0=t0: e.scalar_tensor_tensor(out=o[:, 0:PIECE], in0=u[:, t0 + 2:t0 + 2 + PIECE],
                                                                                        scalar=cw[:, q, 2:3], in1=a[:, 0:PIECE],
                                                                                        op0=ALU.mult, op1=ALU.add), reads=[bu, bw, ba], writes=[bo])
                fw.dma("act", self.UT[q * 128:(q + 1) * 128, t0:t0 + PIECE], o[:, 0:PIECE], reads=[bo], writes=[self.bUT])

    def hy_conv(self, ph, l, s, L):
        fw, nc, H = self.fw, self.nc, self.H
        N, K1, K, nh, kw, C = hy_dims(L)
        CB = 4
        NS = 4
        sel4 = lambda ap, i: ap[:, :, i, :]
        sel5 = lambda ap, i: ap[:, :, :, i, :]

        def stream(st):
            (pA, bpA), (pX, bpX) = [(self.ps[2 * st + i], self.bps[2 * st + i]) for i in range(2)]
            (pB, bpB), (pY, bpY) = (pA, bpA), (pX, bpX)
            gr = self.ring(ph, "hcg%d" % st, [128, 3, CB, 128], BF16, 2)
            yr = self.ring(ph, "hcy%d" % st, [128, CB, 128], BF16, 2)
            kfr = self.ring(ph, "hckf%d" % st, [128, 2, CB, 2 * K1], BF16, 1)
            tar = self.ring(ph, "hcta%d" % st, [128, 512], BF16, 8)
            evr = self.ring(ph, "hcev%d" % st, [128, 512], BF16, 3)
            z1r = self.ring(ph, "hcz1%d" % st, [128, C, 128], BF16, 2)
            g1r = self.ring(ph, "hcg1%d" % st, [128, C * 128], F32, 2)
            g2r = self.ring(ph, "hcg2%d" % st, [128, C * 128], F32, 2)

            def conv(z, bz, xg, bxg, kf, bkf, o, cglob, out, bout):
                for c in range(C):
                    fw.op("pe", lambda e, c=c: e.matmul(pA[:, c * 2 * K1:(c + 1) * 2 * K1], lhsT=z[:, c, :], rhs=H["F1"][:K, :],
                                                        start=True, stop=True), reads=[bz, H["bt"]], writes=[bpA], sig=(c == C - 1))
                yield
                def evt(pat, pat_kw, n, P_):
                    t_, b_ = evr.next()
                    return t_[:P_, 0:n].rearrange("p (" + pat + ") -> p " + pat, **pat_kw), b_

                def prods(shape_pat, **kw_):
                    ta_, bta = tar.next()
                    tb__, btb = tar.next()
                    n_ = 1
                    for v_ in kw_.values():
                        n_ *= v_
                    return ta_, tb__, bta, btb, n_

                ta_, tb__, bta, btb, n_ = prods("c r k", c=C, r=2, k=K1)
                tav = ta_[:, 0:n_].rearrange("p (c r k) -> p c r k", c=C, r=2)
                tbv = tb__[:, 0:n_].rearrange("p (c r k) -> p c r k", c=C, r=2)
                src = pA[:, 0:C * 2 * K1].rearrange("p (c r k) -> p c r k", c=C, r=2)
                bq = Buf()
                self.cprod(src, H["TT"].unsqueeze(1).to_broadcast([128, C, 2, K1]), H["TI"].unsqueeze(1).to_broadcast([128, C, 2, K1]),
                           tav, tbv, [bpA, H["bt"], bta, btb], bq, ev=evt("c r k", pat_kw=dict(c=C, r=2), n=C * 2 * K1, P_=128))
                bta.w = btb.w = bq.w
                yield
                for c in range(C):
                    seq = [("Gr", tav, 0, 0), ("nGr", tbv, 1, 0), ("nGi", tbv, 0, 0), ("nGi", tav, 1, 0),
                           ("Gi", tav, 0, 1), ("nGi", tbv, 1, 1), ("Gr", tbv, 0, 1), ("Gr", tav, 1, 1)]
                    for qi, (g, tv, ri, half) in enumerate(seq):
                        fw.op("pe", lambda e, c=c, g=g, tv=tv, ri=ri, half=half, qi=qi: e.matmul(
                            pX[:, c * 2 * K1 + half * K1:c * 2 * K1 + (half + 1) * K1], lhsT=H[g], rhs=tv[:, c, ri, :],
                            start=(qi % 4 == 0), stop=(qi % 4 == 3)), reads=[bta, btb, H["bt"]], writes=[bpX],
                            sig=(c == C - 1 and qi == 7))
                yield
                ya_, yb_, bya, byb, n_ = prods("c r k", c=C, r=2, k=K1)
                yav = ya_[:, 0:n_].rearrange("p (c r k) -> p c r k", c=C, r=2)
                ybv = yb_[:, 0:n_].rearrange("p (c r k) -> p c r k", c=C, r=2)
                srcx = pX[:, 0:C * 2 * K1].rearrange("p (c r k) -> p c r k", c=C, r=2)
                kfv = kf.rearrange("p c (r k) -> p c r k", r=2)
                bq2 = Buf()
                self.cprod(srcx, kfv[:, :, 0:1, :].to_broadcast([128, C, 2, K1]), kfv[:, :, 1:2, :].to_broadcast([128, C, 2, K1]),
                           yav, ybv, [bpX, bkf, bya, byb], bq2, ev=evt("c r k", pat_kw=dict(c=C, r=2), n=C * 2 * K1, P_=128))
                bya.w = byb.w = bq2.w
                yield
                for c in range(C):
                    for h in range(nh):
                        ob = (c * nh + h) * 256
                        hs = slice(h * kw, (h + 1) * kw)
                        seq = [(yav[:, c, 0, hs], "cG0"), (ybv[:, c, 1, hs], "ncG0"), (ybv[:, c, 0, hs], "cG1"), (yav[:, c, 1, hs], "cG1")]
                        for qi, (lh, g) in enumerate(seq):
                            fw.op("pe", lambda e, lh=lh, g=g, ob=ob, qi=qi: e.matmul(pB[:kw, ob:ob + 256], lhsT=lh, rhs=H[g],
                                                                                   start=(qi == 0), stop=(qi == 3)),
                                  reads=[bya, byb, H["bt"]], writes=[bpB], sig=(c == C - 1 and h == nh - 1 and qi == 3))
                yield
                ba_, bb_, bba, bbb, n_ = prods("h r c m", h=nh, r=2, c=C, m=128)
                bav = ba_[:kw, 0:n_].rearrange("p (h r c m) -> p h r c m", h=nh, r=2, c=C)
                bbv = bb_[:kw, 0:n_].rearrange("p (h r c m) -> p h r c m", h=nh, r=2, c=C)
                srcb = pB[:kw, 0:C * nh * 256].rearrange("p (c h r m) -> p c h r m", c=C, h=nh, r=2)
                bq3 = Buf()
                self.cprod(srcb, H["cTT"][:kw].unsqueeze(1).to_broadcast([kw, C, nh, 2, 128]),
                           H["cTI"][:kw].unsqueeze(1).to_broadcast([kw, C, nh, 2, 128]),
                           bav.rearrange("p h r c m -> p c h r m"), bbv.rearrange("p h r c m -> p c h r m"),
                           [bpB, H["bt"], bba, bbb], bq3, ev=evt("c h r m", pat_kw=dict(c=C, h=nh, r=2), n=C * nh * 256, P_=kw))
                bba.w = bbb.w = bq3.w
                yield
                for h in range(nh):
                    seq = [("cHr", bav, 0), ("ncHr", bbv, 1), ("ncHi", bbv, 0), ("ncHi", bav, 1)]
                    for qi, (g, tv, ri) in enumerate(seq):
                        fw.op("pe", lambda e, h=h, g=g, tv=tv, ri=ri, qi=qi: e.matmul(
                            pY[:K, 0:C * 128], lhsT=H[g][:kw, h, :], rhs=tv[:, h, ri, :, :].rearrange("p c m -> p (c m)"),
                            start=(h == 0 and qi == 0), stop=(h == nh - 1 and qi == 3)),
                            reads=[bba, bbb, H["bt"]], writes=[bpY], sig=(h == nh - 1 and qi == 3))
                yield
                g1, bg1 = g1r.next()
                g2, bg2 = g2r.next()
                g1v = g1[:K, :].rearrange("p (c m) -> p c m", c=C)
                g2v = g2[:K, :].rearrange("p (c m) -> p c m", c=C)
                yv = pY[:K, 0:C * 128].rearrange("p (c m) -> p c m", c=C)
                rinv = H["rinv"][:K, o, cglob:cglob + C].unsqueeze(2).to_broadcast([K, C, 128])
                skp = H["skip"][:K, o, cglob:cglob + C].unsqueeze(2).to_broadcast([K, C, 128])
                fw.op("dve", lambda e: e.tensor_tensor(out=g1v, in0=yv, in1=rinv, op=ALU.mult), reads=[bpY, H["brs"]], writes=[bg1])
                fw.op("pool", lambda e: e.tensor_tensor(out=g2v, in0=z, in1=skp, op=ALU.mult), reads=[bz, H["brs"]], writes=[bg2])
                fw.op("pool", lambda e: e.tensor_tensor(out=g1v, in0=g1v, in1=g2v, op=ALU.add), reads=[bg1, bg2], writes=[bg1])
                fw.op("dve", lambda e: e.tensor_tensor(out=out, in0=g1v, in1=xg, op=ALU.mult), reads=[bg1, bxg], writes=[bout])
                yield

            groups = [g for g in range(512 // CB) if g % NS == st]
            for g in groups:
                c0 = g * CB
                gt, bgt = gr.next()
                for j in range(3):
                    fw.dma("sp", gt[:K, j, :, :], self.UT[j * 512 + c0:j * 512 + c0 + CB, 0:L].rearrange("c (a b) -> a c b", b=128),
                           reads=[self.bUT], writes=[bgt])
                kf, bkf = kfr.next()
                for o in range(2):
                    fw.dma("sp", kf[:, o, :, :], self.KF[o, :, c0:c0 + CB, 0:2 * K1], reads=[self.bKF], writes=[bkf])
                yt_, byt_ = yr.next()
                for cc in range(0, CB, C):
                    z1, bz1 = z1r.next()
                    yield from conv(gt[:K, 0, cc:cc + C, :], bgt, gt[:K, 1, cc:cc + C, :], bgt, kf[:, 0, cc:cc + C, :], bkf, 0,
                                    c0 + cc, z1[:K, :, :], bz1)
                    yield from conv(z1[:K, :, :], bz1, gt[:K, 2, cc:cc + C, :], bgt, kf[:, 1, cc:cc + C, :], bkf, 1,
                                    c0 + cc, yt_[:K, cc:cc + C, :], byt_)
                fw.dma("act", self.YT[c0:c0 + CB, 0:L].rearrange("c (a b) -> a c b", b=128), yt_[:K, :, :], reads=[byt_], writes=[self.bYT])
                yield

        run_window((stream(st) for st in range(NS)), NS)

    Prog.cmul = cmul
    Prog.cprod = cprod
    Prog.hyena = hyena
    Prog.hy_filter = hy_filter
    Prog.hy_filter_fft = hy_filter_fft
    Prog.hy_shortconv = hy_shortconv
    Prog.hy_conv = hy_conv


_hyena_methods()

LS = (8192, 16384)
_CACHE = {}


def _program():
    if "p" not in _CACHE:
        P = Prog(list(LS), depth=DEPTH, debug=False)
        P.build()
        _CACHE["p"] = P
    return _CACHE["p"]


def kernel(x_prompt, x_sample, c_prompt, c_sample, w_ada, b_ada, w_in, b_in, hy_conv_w, hy_conv_b,
           hy_w1, hy_b1, hy_freq, hy_w2, hy_b2, hy_w3, hy_b3, hy_decay, hy_skip, na_rpb,
           w_branch_a, w_branch_b, w_branch_c, w_out, ln_g, ln_b):
    f = lambda a: np.ascontiguousarray(np.asarray(a, dtype=np.float32))
    P = _program()
    raw = dict(w_ada=w_ada, b_ada=b_ada, w_in=w_in, b_in=b_in, hy_conv_w=hy_conv_w, hy_conv_b=hy_conv_b, hy_w1=hy_w1,
               hy_b1=hy_b1, hy_freq=hy_freq, hy_w2=hy_w2, hy_b2=hy_b2, hy_w3=hy_w3, hy_b3=hy_b3, hy_decay=hy_decay,
               hy_skip=hy_skip, w_branch_a=w_branch_a, w_branch_b=w_branch_b, w_branch_c=w_branch_c, w_out=w_out,
               ln_g=ln_g, ln_b=ln_b)
    shared = layout_weights(raw, DEPTH)
    shared["dmask"] = band_masks()
    shared["hyG"] = hyena_g_table()
    rpb = f(na_rpb)
    for s, L in enumerate(LS):
        shared["rope%d" % s] = rope_table(L)
        shared["nab%d" % s] = Prog.na_values(L, rpb)
        shared["hz%d" % s] = hyena_pos(L)
        a32, abf = hyena_tables(L)
        shared["hyf32_%d" % s] = a32
        shared["hybf_%d" % s] = abf
    xp, xs, cp, cs = f(x_prompt), f(x_sample), f(c_prompt), f(c_sample)
    in_maps = []
    for core in range(8):
        m = dict(shared)
        ip, is_ = core % 4, core % 2
        m["x0"], m["c0"] = xp[ip], np.ascontiguousarray(cp[ip].reshape(8, 128).T)
        m["x1"], m["c1"] = xs[is_], np.ascontiguousarray(cs[is_].reshape(8, 128).T)
        in_maps.append(m)
    res = run_bass_kernel_spmd(P.nc, in_maps, core_ids=list(range(8)))
    y_prompt = np.stack([res.results[i]["y0"] for i in range(4)], 0).astype(np.float32)
    y_sample = np.stack([res.results[i]["y1"] for i in range(2)], 0).astype(np.float32)
    return (y_prompt, y_sample)
```
